# Optimizing a Trainium2 kernel written in Bass

```python
import math
import jax, jax.numpy as jnp
from jax import lax
import numpy as np

D_MODEL = 2048
BATCH = 4
SEQ = 2048
DEPTH = 1
DEC_BATCH = 128
DEC_SEQ = 1
PAST_LEN = 16384
PAGE_SIZE = 128

S5_WIDTH = 1024
S5_GROUP = 16
S5_GROUPS = S5_WIDTH // S5_GROUP
S5_STATE = 64
DT_MIN = 1e-3
DT_MAX = 1e-1
HG_WIDTH = 1024
HG_HEADS = 8
HG_DK = HG_WIDTH // HG_HEADS
HG_DV = HG_WIDTH // HG_HEADS
HG_CHUNK = 64
IN_WIDTH = S5_WIDTH + 4 * HG_WIDTH + 2 * D_MODEL
D_FF = 5632
EPS = 1e-6

kernel_name = "hybrid_s5_hgrn2_macaron_decode_step"


def rmsnorm(x, g):
    xf = x.astype(jnp.float32)
    y = xf * lax.rsqrt(jnp.mean(xf * xf, axis=-1, keepdims=True) + EPS)
    return y * g.astype(jnp.float32)


def swiglu(h, w_gate, w_up, w_down):
    return (jax.nn.silu(h @ w_gate) * (h @ w_up)) @ w_down


def s5_mixer(u, h0_re, h0_im, lam_re, lam_im, log_dt, b_re, b_im, c_re, c_im, d_skip, w_glu, b_glu):
    f32 = jnp.float32
    bsz, L, _ = u.shape
    ug = u.reshape(bsz, L, S5_GROUPS, S5_GROUP).astype(jnp.complex64)
    lam = lax.complex(lam_re.astype(f32), lam_im.astype(f32))
    dt = jnp.exp(log_dt.astype(f32))[:, None]
    a_bar = jnp.exp(lam * dt)
    b = lax.complex(b_re.astype(f32), b_im.astype(f32))
    b_bar = ((a_bar - 1.0) / lam)[..., None] * b
    bu = jnp.einsum('gpn,blgn->blgp', b_bar, ug)
    h0 = lax.complex(h0_re.astype(f32), h0_im.astype(f32))
    bu = bu.at[:, 0].add(a_bar * h0)
    a = jnp.broadcast_to(a_bar, bu.shape)

    def combine(e1, e2):
        a1, b1 = e1
        a2, b2 = e2
        return a1 * a2, a2 * b1 + b2

    _, h = lax.associative_scan(combine, (a, bu), axis=1)
    c = lax.complex(c_re.astype(f32), c_im.astype(f32))
    y = jnp.real(jnp.einsum('gnp,blgp->blgn', c, h)).reshape(bsz, L, S5_WIDTH)
    y = y + d_skip.astype(f32) * u
    z = jax.nn.gelu(y)
    out = z * jax.nn.sigmoid(z @ w_glu.astype(f32) + b_glu.astype(f32))
    h_last = h[:, -1]
    return out, jnp.real(h_last), jnp.imag(h_last)


def hgrn2_mixer(q, f_logit, inp, gate, lb, S0, g_norm):
    bsz, L, _ = q.shape
    f = lb + (1.0 - lb) * jax.nn.sigmoid(f_logit)
    log_f = jnp.log(f)
    k = 1.0 - f
    q = jax.nn.silu(q)
    chunk = min(HG_CHUNK, L)
    n_chunks = -(-L // chunk)
    pad = n_chunks * chunk - L

    def to_chunks(t, d):
        t = jnp.pad(t, ((0, 0), (0, pad), (0, 0)))
        return t.reshape(bsz, n_chunks, chunk, HG_HEADS, d).transpose(1, 0, 3, 2, 4)

    qc, kc, gc = to_chunks(q, HG_DK), to_chunks(k, HG_DK), to_chunks(log_f, HG_DK)
    ic = to_chunks(inp, HG_DV)
    mask = jnp.tril(jnp.ones((chunk, chunk), dtype=bool))[:, :, None]

    def step(S, xs):
        qb, kb, gb, ib = xs
        G = jnp.cumsum(gb, axis=2)
        o_inter = jnp.einsum('bhtk,bhkv->bhtv', qb * jnp.exp(G), S)
        diff = G[:, :, :, None, :] - G[:, :, None, :, :]
        decay = jnp.exp(jnp.where(mask, diff, -jnp.inf))
        att = jnp.einsum('bhtk,bhtsk,bhsk->bhts', qb, decay, kb)
        o = o_inter + jnp.einsum('bhts,bhsv->bhtv', att, ib)
        G_last = G[:, :, -1]
        S_new = jnp.exp(G_last)[..., None] * S + jnp.einsum(
            'bhsk,bhsv->bhkv', kb * jnp.exp(G_last[:, :, None] - G), ib)
        return S_new, o

    S_fin, oc = lax.scan(step, S0, (qc, kc, gc, ic))
    o = oc.transpose(1, 0, 3, 2, 4).reshape(bsz, n_chunks * chunk, HG_HEADS, HG_DV)[:, :L]
    o = rmsnorm(o, g_norm.reshape(HG_HEADS, HG_DV)).reshape(bsz, L, HG_WIDTH)
    return o * jax.nn.silu(gate), S_fin


def decoder_layer(x, s5_re, s5_im, S0, lb, p):
    dt = x.dtype
    f32 = jnp.float32
    h = rmsnorm(x, p['ffn1_pre_norm']).astype(dt)
    x = x + (0.5 * rmsnorm(swiglu(h, p['ffn1_w_gate'], p['ffn1_w_up'], p['ffn1_w_down']), p['ffn1_post_norm'])).astype(dt)
    h = rmsnorm(x, p['mix_pre_norm']).astype(dt)
    proj = (h @ p['w_in']).astype(f32)
    cuts = [S5_WIDTH, S5_WIDTH + HG_WIDTH, S5_WIDTH + 2 * HG_WIDTH, S5_WIDTH + 3 * HG_WIDTH,
            S5_WIDTH + 4 * HG_WIDTH, S5_WIDTH + 4 * HG_WIDTH + D_MODEL]
    u, q, f_logit, inp, og, g_s5, g_hg = jnp.split(proj, cuts, axis=-1)
    s5_out, s5_re_new, s5_im_new = s5_mixer(
        u, s5_re, s5_im, p['s5_lambda_re'], p['s5_lambda_im'], p['s5_log_dt'], p['s5_b_re'], p['s5_b_im'],
        p['s5_c_re'], p['s5_c_im'], p['s5_d'], p['s5_w_glu'], p['s5_b_glu'])
    hg_out, S_new = hgrn2_mixer(q, f_logit, inp, og, lb, S0.astype(f32), p['hgrn_out_norm'])
    merged = (jax.nn.sigmoid(g_s5) * (s5_out @ p['w_branch_s5'].astype(f32))
              + jax.nn.sigmoid(g_hg) * (hg_out @ p['w_branch_hgrn'].astype(f32)))
    mix = merged.astype(dt) @ p['w_out']
    x = x + rmsnorm(mix, p['mix_post_norm']).astype(dt)
    h = rmsnorm(x, p['ffn2_pre_norm']).astype(dt)
    x = x + (0.5 * rmsnorm(swiglu(h, p['ffn2_w_gate'], p['ffn2_w_up'], p['ffn2_w_down']), p['ffn2_post_norm'])).astype(dt)
    return x, s5_re_new, s5_im_new, S_new


def setup_inputs(seed: int = 0) -> dict:
    key = jax.random.key(seed)
    ks = iter(jax.random.split(key, 48))
    f32 = jnp.float32

    def nrm(shape, scale):
        return scale * jax.random.normal(next(ks), shape, f32)

    def gain(shape):
        return 1.0 + 0.02 * jax.random.normal(next(ks), shape, f32)

    L_, G, P, N = DEPTH, S5_GROUPS, S5_STATE, S5_GROUP
    lam_im_base = jnp.broadcast_to(math.pi * jnp.arange(P, dtype=f32), (L_, G, P))
    return {
        'x_prompt': nrm((BATCH, SEQ, D_MODEL), 1.0),
        'x_sample': nrm((DEC_BATCH, DEC_SEQ, D_MODEL), 1.0),
        'state_s5_re': nrm((L_, DEC_BATCH, G, P), 0.3),
        'state_s5_im': nrm((L_, DEC_BATCH, G, P), 0.3),
        'state_hgrn': nrm((L_, DEC_BATCH, HG_HEADS, HG_DK, HG_DV), 0.3),
        'ffn1_pre_norm': gain((L_, D_MODEL)),
        'ffn1_w_gate': nrm((L_, D_MODEL, D_FF), D_MODEL ** -0.5),
        'ffn1_w_up': nrm((L_, D_MODEL, D_FF), D_MODEL ** -0.5),
        'ffn1_w_down': nrm((L_, D_FF, D_MODEL), D_FF ** -0.5),
        'ffn1_post_norm': gain((L_, D_MODEL)),
        'mix_pre_norm': gain((L_, D_MODEL)),
        'w_in': nrm((L_, D_MODEL, IN_WIDTH), D_MODEL ** -0.5),
        's5_lambda_re': -0.5 + nrm((L_, G, P), 0.01),
        's5_lambda_im': lam_im_base + nrm((L_, G, P), 0.01),
        's5_log_dt': jax.random.uniform(next(ks), (L_, G), f32, math.log(DT_MIN), math.log(DT_MAX)),
        's5_b_re': nrm((L_, G, P, N), (2 * N) ** -0.5),
        's5_b_im': nrm((L_, G, P, N), (2 * N) ** -0.5),
        's5_c_re': nrm((L_, G, N, P), (2 * P) ** -0.5),
        's5_c_im': nrm((L_, G, N, P), (2 * P) ** -0.5),
        's5_d': nrm((L_, S5_WIDTH), 1.0),
        's5_w_glu': nrm((L_, S5_WIDTH, S5_WIDTH), S5_WIDTH ** -0.5),
        's5_b_glu': nrm((L_, S5_WIDTH), 0.02),
        'hgrn_lb_logits': nrm((L_ + 1, HG_WIDTH), 0.5),
        'hgrn_out_norm': gain((L_, HG_WIDTH)),
        'w_branch_s5': nrm((L_, S5_WIDTH, D_MODEL), S5_WIDTH ** -0.5),
        'w_branch_hgrn': nrm((L_, HG_WIDTH, D_MODEL), HG_WIDTH ** -0.5),
        'w_out': nrm((L_, D_MODEL, D_MODEL), D_MODEL ** -0.5),
        'mix_post_norm': gain((L_, D_MODEL)),
        'ffn2_pre_norm': gain((L_, D_MODEL)),
        'ffn2_w_gate': nrm((L_, D_MODEL, D_FF), D_MODEL ** -0.5),
        'ffn2_w_up': nrm((L_, D_MODEL, D_FF), D_MODEL ** -0.5),
        'ffn2_w_down': nrm((L_, D_FF, D_MODEL), D_FF ** -0.5),
        'ffn2_post_norm': gain((L_, D_MODEL)),
    }


def reference(x_prompt, x_sample, state_s5_re, state_s5_im, state_hgrn,
              ffn1_pre_norm, ffn1_w_gate, ffn1_w_up, ffn1_w_down, ffn1_post_norm,
              mix_pre_norm, w_in, s5_lambda_re, s5_lambda_im, s5_log_dt, s5_b_re, s5_b_im,
              s5_c_re, s5_c_im, s5_d, s5_w_glu, s5_b_glu, hgrn_lb_logits, hgrn_out_norm,
              w_branch_s5, w_branch_hgrn, w_out, mix_post_norm,
              ffn2_pre_norm, ffn2_w_gate, ffn2_w_up, ffn2_w_down, ffn2_post_norm):
    f32 = jnp.float32
    lb_all = jnp.cumsum(jax.nn.softmax(hgrn_lb_logits.astype(f32), axis=0), axis=0)
    bp = x_prompt.shape[0]
    yp, ys = x_prompt, x_sample
    p_re, p_im, p_hg, s_re, s_im, s_hg = [], [], [], [], [], []
    for l in range(DEPTH):
        p = {
            'ffn1_pre_norm': ffn1_pre_norm[l], 'ffn1_w_gate': ffn1_w_gate[l], 'ffn1_w_up': ffn1_w_up[l],
            'ffn1_w_down': ffn1_w_down[l], 'ffn1_post_norm': ffn1_post_norm[l],
            'mix_pre_norm': mix_pre_norm[l], 'w_in': w_in[l],
            's5_lambda_re': s5_lambda_re[l], 's5_lambda_im': s5_lambda_im[l], 's5_log_dt': s5_log_dt[l],
            's5_b_re': s5_b_re[l], 's5_b_im': s5_b_im[l], 's5_c_re': s5_c_re[l], 's5_c_im': s5_c_im[l],
            's5_d': s5_d[l], 's5_w_glu': s5_w_glu[l], 's5_b_glu': s5_b_glu[l],
            'hgrn_out_norm': hgrn_out_norm[l], 'w_branch_s5': w_branch_s5[l],
            'w_branch_hgrn': w_branch_hgrn[l], 'w_out': w_out[l], 'mix_post_norm': mix_post_norm[l],
            'ffn2_pre_norm': ffn2_pre_norm[l], 'ffn2_w_gate': ffn2_w_gate[l], 'ffn2_w_up': ffn2_w_up[l],
            'ffn2_w_down': ffn2_w_down[l], 'ffn2_post_norm': ffn2_post_norm[l],
        }
        lb = lb_all[l]
        z_s5 = jnp.zeros((bp, S5_GROUPS, S5_STATE), f32)
        z_hg = jnp.zeros((bp, HG_HEADS, HG_DK, HG_DV), f32)
        yp, a_re, a_im, a_hg = decoder_layer(yp, z_s5, z_s5, z_hg, lb, p)
        ys, b_re, b_im, b_hg = decoder_layer(ys, state_s5_re[l], state_s5_im[l], state_hgrn[l], lb, p)
        p_re.append(a_re); p_im.append(a_im); p_hg.append(a_hg)
        s_re.append(b_re); s_im.append(b_im); s_hg.append(b_hg)
    new_s5_re_p = jnp.stack(p_re)
    new_s5_im_p = jnp.stack(p_im)
    new_hgrn_p = jnp.stack(p_hg)
    new_s5_re_s = jnp.stack(s_re)
    new_s5_im_s = jnp.stack(s_im)
    new_hgrn_s = jnp.stack(s_hg)
    return (yp, ys, new_s5_re_p, new_s5_im_p, new_hgrn_p, new_s5_re_s, new_s5_im_s, new_hgrn_s)
```

```python
import math
from contextlib import ExitStack

import numpy as np
import concourse.bass as bass
import concourse.mybir as mybir
from concourse.bass_utils import run_bass_kernel_spmd

F32 = mybir.dt.float32
BF16 = mybir.dt.bfloat16
I32 = mybir.dt.int32
AF = mybir.ActivationFunctionType
ALU = mybir.AluOpType

D = 2048
DFF = 5632
NPASS = 4
NA = 2
NB = 2
TB = 512
NS = 16
NT = TB + NS
EPS = 1e-6
STAGE = 99
DEBUG = False
POOL_POST = False
SKIP = set()


class Tok:
    __slots__ = ("sem", "val")

    def __init__(self, sem, val):
        self.sem = sem
        self.val = val


class Buf:
    __slots__ = ("w", "rs", "t", "excl")

    def __init__(self, t=None, excl=False):
        self.w = None
        self.rs = {}
        self.t = t
        self.excl = excl


class Eng:
    def __init__(self, e, sem):
        self.e = e
        self.sem = sem
        self.cnt = 0
        self.seen = {}

    def wait(self, tok):
        if tok is None:
            return
        k = id(tok.sem)
        if self.seen.get(k, 0) >= tok.val:
            return
        self.e.wait_ge(tok.sem, tok.val)
        self.seen[k] = tok.val

    def sig(self, ins):
        ins.then_inc(self.sem, 1)
        self.cnt += 1
        return Tok(self.sem, self.cnt)


class Chan:
    def __init__(self, sem):
        self.sem = sem
        self.cnt = 0


def _deps(E, reads, writes):
    for b in reads:
        E.wait(b.w)
        if b.excl:
            for r in b.rs.values():
                E.wait(r)
    for b in writes:
        E.wait(b.w)
        for r in b.rs.values():
            E.wait(r)


def _reg(tok, reads, writes):
    for b in reads:
        if b.excl:
            b.w = tok
            b.rs = {}
        else:
            b.rs[id(tok.sem)] = tok
    for b in writes:
        b.w = tok
        b.rs = {}


def run(E, fn, reads=(), writes=()):
    _deps(E, reads, writes)
    tok = E.sig(fn())
    _reg(tok, reads, writes)
    return tok


def build_program(npass=NB, stage=STAGE, na=NA):
    nc = bass.Bass("TRN2", target_bir_lowering=False)
    es = ExitStack()

    def din(name, shape):
        return nc.dram_tensor(name, list(shape), F32, kind="ExternalInput").ap()

    def dout(name, shape):
        return nc.dram_tensor(name, list(shape), F32, kind="ExternalOutput").ap()

    xa = din("xa", [NA * TB, D])
    xp = din("xp", [NB * TB, D])
    xs = din("xs", [NS, D])
    sre = din("sre", [NS, 4096])
    sim = din("sim", [NS, 4096])
    shg = din("shg", [NS, 8, 128, 128])
    f1g = din("f1g", [D, DFF]); f1u = din("f1u", [D, DFF]); f1d = din("f1d", [DFF, D])
    f2g = din("f2g", [D, DFF]); f2u = din("f2u", [D, DFF]); f2d = din("f2d", [DFF, D])
    win = din("win", [D, 9216])
    wglu = din("wglu", [1024, 1024])
    wbs = din("wbs", [1024, D]); wbh = din("wbh", [1024, D]); wout = din("wout", [D, D])
    gains_d = din("gains", [128, 96])
    misc_d = din("misc", [128, 40])
    lam_d = din("lam", [128, 96])
    bblk_d = din("bblk", [8, 128, 1024])
    cblk_d = din("cblk", [8, 4, 128, 256])
    ident_d = din("ident", [128, 128])
    mask_d = din("mask", [128, 128])
    scanm_d = din("scanm", [128, TB])
    oneh_d = din("oneh", [NS, NS * 128])

    dbg_s5 = dout("dbg_s5", [128, 8, NT]) if DEBUG else None
    dbg_hg = dout("dbg_hg", [128, 8, NT]) if DEBUG else None
    yp = dout("yp", [NB * TB, D])
    ys = dout("ys", [NS, D])
    o_re_p = dout("o_re_p", [32, 128]); o_im_p = dout("o_im_p", [32, 128])
    o_hg_p = dout("o_hg_p", [8, 128, 128])
    o_re_s = dout("o_re_s", [NS, 4096]); o_im_s = dout("o_im_s", [NS, 4096])
    o_hg_s = dout("o_hg_s", [NS, 8, 128, 128])

    with es:
        def sem(name):
            return es.enter_context(nc.semaphore(name))

        def sb(name, shape, dt=F32):
            return es.enter_context(nc.sbuf_tensor("sb_" + name, list(shape), dt))

        PE = Eng(nc.tensor, sem("s_pe"))
        ACT = Eng(nc.scalar, sem("s_act"))
        DVE = Eng(nc.vector, sem("s_dve"))
        POOL = Eng(nc.gpsimd, sem("s_pool"))
        SP = Eng(nc.sync, sem("s_sp"))
        chans = []

        def chan(name):
            c = Chan(sem(name))
            chans.append(c)
            return c

        def dma(Q, ch, out, in_, reads=(), writes=(), **kw):
            _deps(Q, reads, writes)
            ins = Q.e.dma_start(out=out, in_=in_, **kw)
            ins.then_inc(ch.sem, 16)
            ch.cnt += 16
            tok = Tok(ch.sem, ch.cnt)
            _reg(tok, reads, writes)
            return tok

        banks = []
        for i in range(8):
            t = es.enter_context(nc.psum_tensor(f"ps{i}", [128, 512], F32))
            banks.append(Buf(t, excl=True))
        rr = [0]

        def pbank():
            b = banks[rr[0] % 4]
            rr[0] += 1
            return b
        srr = [0]

        def sslot():
            b = banks[4 + srr[0] % 2]
            srr[0] += 1
            return b, 0
        LBANK = banks[6]
        LBANK2 = banks[7]

        def mm(wbuf, out_ap, steps, reads, start=True, stop=True, signal=True):
            wbufs = wbuf if isinstance(wbuf, list) else [wbuf]
            _deps(PE, reads, wbufs)
            n = len(steps)
            ins = None
            for i, (l, r) in enumerate(steps):
                ins = nc.tensor.matmul(out_ap, lhsT=l, rhs=r, start=(start and i == 0),
                                       stop=(stop and i == n - 1))
            if signal:
                tok = PE.sig(ins)
                _reg(tok, reads, wbufs)
                return tok
            return None

        def tr(wbuf, out_ap, in_ap, ident_ap, reads, signal=True):
            _deps(PE, reads, [wbuf])
            ins = nc.tensor.transpose(out=out_ap, in_=in_ap, identity=ident_ap)
            if signal:
                tok = PE.sig(ins)
                _reg(tok, reads, [wbuf])
                return tok
            return None

        def barrier():
            engs = (PE, ACT, DVE, POOL, SP)
            for E in engs:
                for E2 in engs:
                    if E2 is not E and E2.cnt:
                        E.wait(Tok(E2.sem, E2.cnt))
                for c in chans:
                    if c.cnt:
                        E.wait(Tok(c.sem, c.cnt))

        def copy(E, out, in_, reads, writes):
            if E is ACT:
                return run(ACT, lambda: nc.scalar.copy(out=out, in_=in_), reads, writes)
            return run(E, lambda: E.e.tensor_copy(out=out, in_=in_), reads, writes)

        cch = chan("c_const")
        cch2 = chan("c_const2")
        gains = sb("gains", [128, 96]); B_gains = Buf()
        misc = sb("misc", [128, 40]); B_misc = Buf()
        lam = sb("lam", [128, 96]); B_lam = Buf()
        ident = sb("ident", [128, 128]); B_ident = Buf()
        identb = sb("identb", [128, 128], BF16); B_identb = Buf()
        onesb = sb("onesb", [128, 128], BF16); B_onesb = Buf()
        mask = sb("mask", [128, 128]); B_mask = Buf()
        scanm = sb("scanm", [128, TB], BF16); B_scanm = Buf()
        for (t, d, b) in ((gains, gains_d, B_gains), (misc, misc_d, B_misc), (lam, lam_d, B_lam),
                          (ident, ident_d, B_ident), (mask, mask_d, B_mask), (scanm, scanm_d, B_scanm)):
            if t is scanm:
                dma(POOL, cch2, t[:], d, writes=[b])
            else:
                dma(SP, cch, t[:], d, writes=[b])
        for b in (B_gains, B_misc, B_lam, B_ident, B_mask):
            b.w = Tok(cch.sem, cch.cnt)
        run(DVE, lambda: nc.vector.tensor_copy(out=identb[:], in_=ident[:]), [B_ident], [B_identb])
        run(DVE, lambda: nc.vector.memset(onesb[:], 1.0), [], [B_onesb])

        def gain(i):
            return gains[:, i * 16:(i + 1) * 16]

        ghalf = sb("ghalf", [128, 32]); B_ghalf = Buf()
        run(DVE, lambda: nc.vector.tensor_scalar(out=ghalf[:, 0:16], in0=gain(1), scalar1=0.5, scalar2=None,
                                                 op0=ALU.mult), [B_gains], [B_ghalf])
        run(DVE, lambda: nc.vector.tensor_scalar(out=ghalf[:, 16:32], in0=gain(5), scalar1=0.5, scalar2=None,
                                                 op0=ALU.mult), [B_gains], [B_ghalf])

        NSLOT = 4
        SLOTE = 4096
        wslots = []
        for i in range(NSLOT):
            wslots.append((sb(f"wslot{i}", [128, SLOTE], BF16), Buf(), chan(f"c_w{i}")))
        wrr = [0]
        xw_ch = [chan(f"c_wx{i}") for i in range(3)]

        def wload(W, r0, nkt, c0, ncols):
            t, b, ch = wslots[wrr[0] % len(wslots)]
            wrr[0] += 1
            assert nkt * ncols <= SLOTE
            view = t[:, 0:nkt * ncols].rearrange("p (k m) -> p k m", m=ncols)
            k0 = 0
            while k0 < nkt:
                kk = min(16, nkt - k0)
                src = W[r0 + k0 * 128:r0 + (k0 + kk) * 128, c0:c0 + ncols].rearrange("(k p) m -> p k m", p=128)
                dma(POOL, ch, view[:, k0:k0 + kk, :], src, writes=[b])
                k0 += kk
            return b, view

        xT = sb("xT", [128, 16, NT]); B_x = [Buf() for _ in range(16)]
        hT = sb("hT", [128, 16, NT], BF16); B_h = [Buf() for _ in range(16)]
        rstd = sb("rstd", [128, NT]); B_rstd = Buf()
        sqt = sb("sqt", [128, NT], BF16); B_sq = Buf()
        sqt2 = sb("sqt2", [128, NT], BF16); B_sq2 = Buf()
        tmpA = sb("tmpA", [128, NT]); B_tmpA = Buf()
        tmpB = sb("tmpB", [128, NT]); B_tmpB = Buf()

        def blocks(p):
            return [(0, TB)] + ([(TB, NS)] if p == 0 else [])

        def ncols(p):
            return NT if p == 0 else TB

        def rms_rstd(p, src, Bsrc, ntile):
            NC_ = ncols(p)
            ps = LBANK
            for kt in range(ntile):
                sq_, Bq_ = (sqt, B_sq) if kt % 2 == 0 else (sqt2, B_sq2)
                run(ACT, lambda: nc.scalar.activation(out=sq_[:, 0:NC_], in_=src[:, kt, 0:NC_], func=AF.Square),
                    [Bsrc[kt]], [Bq_])
                if p == 0:
                    mm(ps, ps.t[:, 0:TB], [(onesb[:], sq_[:, 0:TB])], [Bq_, B_onesb], start=(kt == 0),
                       stop=(kt == ntile - 1), signal=False)
                    mm([LBANK2, ps], LBANK2.t[:, 0:NS], [(onesb[:], sq_[:, TB:NT])], [Bq_, B_onesb],
                       start=(kt == 0), stop=(kt == ntile - 1))
                else:
                    mm(ps, ps.t[:, 0:TB], [(onesb[:], sq_[:, 0:TB])], [Bq_, B_onesb], start=(kt == 0),
                       stop=(kt == ntile - 1))
            dn = float(ntile * 128)
            for (c0, n) in blocks(p):
                src_ps = ps if c0 == 0 else LBANK2
                sl = src_ps.t[:, 0:n]
                run(DVE, lambda: nc.vector.tensor_scalar(out=rstd[:, c0:c0 + n], in0=sl, scalar1=1.0 / dn,
                                                         scalar2=EPS, op0=ALU.mult, op1=ALU.add),
                    [src_ps], [B_rstd])
            run(ACT, lambda: nc.scalar.activation(out=rstd[:, 0:NC_], in_=rstd[:, 0:NC_], func=AF.Sqrt),
                [B_rstd], [B_rstd])
            run(DVE, lambda: nc.vector.reciprocal(out=rstd[:, 0:NC_], in_=rstd[:, 0:NC_]), [B_rstd], [B_rstd])

        def prenorm(p, gi):
            NC_ = ncols(p)
            rms_rstd(p, xT, B_x, 16)
            g = gain(gi)
            for kt in range(16):
                run(DVE, lambda: nc.vector.scalar_tensor_tensor(
                    out=hT[:, kt, 0:NC_], in0=xT[:, kt, 0:NC_], scalar=g[:, kt:kt + 1], in1=rstd[:, 0:NC_],
                    op0=ALU.mult, op1=ALU.mult), [B_x[kt], B_rstd, B_gains], [B_h[kt]])

        def dense_tile(p, chunks, rhs, consume):
            outs = []
            for (c0, n) in blocks(p):
                if c0 == 0:
                    pb = pbank(); oap = pb.t[:, 0:n]
                else:
                    pb, s0 = sslot(); oap = pb.t[:, s0:s0 + n]
                nch = len(chunks)
                reads = []
                for ci, (wb, wv, k0) in enumerate(chunks):
                    nk = wv.shape[1]
                    steps = []
                    reads.append(wb)
                    for k in range(nk):
                        ra, rb = rhs(k0 + k)
                        steps.append((wv[:, k, :], ra[:, c0:c0 + n]))
                        reads.append(rb)
                    mm(pb, oap, steps, reads, start=(ci == 0), stop=(ci == nch - 1), signal=(ci == nch - 1))
                outs.append((c0, n, pb, oap))
            consume(outs)

        def ffn(p, wg, wu, wd, g_pre, ghalf_off):
            NC_ = ncols(p)
            with ExitStack() as fs:
                aT = fs.enter_context(nc.sbuf_tensor(f"aT{p}{ghalf_off}", [128, 44, NT], BF16))
                B_a = [Buf() for _ in range(44)]
                foT = fs.enter_context(nc.sbuf_tensor(f"foT{p}{ghalf_off}", [128, 16, NT], F32))
                B_fo = [Buf() for _ in range(16)]
                prenorm(p, g_pre)
                hr = lambda k: (hT[:, k, :], B_h[k])
                for c in range(22):
                    bg, vg = wload(wg, 0, 16, c * 256, 256)
                    bu, vu = wload(wu, 0, 16, c * 256, 256)
                    for j in range(2):
                        ft = c * 2 + j
                        res = {}
                        dense_tile(p, [(bg, vg[:, :, j * 128:(j + 1) * 128], 0)], hr, lambda o: res.__setitem__("g", o))
                        dense_tile(p, [(bu, vu[:, :, j * 128:(j + 1) * 128], 0)], hr, lambda o: res.__setitem__("u", o))
                        for (og, ou) in zip(res["g"], res["u"]):
                            c0, n, pbg, apg = og
                            _, _, pbu, apu = ou
                            run(ACT, lambda: nc.scalar.activation(out=tmpA[:, c0:c0 + n], in_=apg, func=AF.Silu),
                                [pbg], [B_tmpA])
                            run(DVE, lambda: nc.vector.tensor_tensor(out=aT[:, ft, c0:c0 + n], in0=tmpA[:, c0:c0 + n],
                                                                     in1=apu, op=ALU.mult), [B_tmpA, pbu], [B_a[ft]])
                ar = lambda k: (aT[:, k, :], B_a[k])
                for dt in range(16):
                    b0, v0 = wload(wd, 0, 22, dt * 128, 128)
                    b1, v1 = wload(wd, 22 * 128, 22, dt * 128, 128)

                    def cons(outs, dt=dt):
                        for (c0, n, pb, ap) in outs:
                            run(ACT, lambda: nc.scalar.copy(out=foT[:, dt, c0:c0 + n], in_=ap), [pb], [B_fo[dt]])
                    dense_tile(p, [(b0, v0, 0), (b1, v1, 22)], ar, cons)
                rms_rstd(p, foT, B_fo, 16)
                for dt in range(16):
                    run(DVE, lambda: nc.vector.scalar_tensor_tensor(
                        out=tmpA[:, 0:NC_], in0=foT[:, dt, 0:NC_], scalar=ghalf[:, ghalf_off + dt:ghalf_off + dt + 1],
                        in1=rstd[:, 0:NC_], op0=ALU.mult, op1=ALU.mult), [B_fo[dt], B_rstd, B_ghalf], [B_tmpA])
                    run(DVE, lambda: nc.vector.tensor_tensor(out=xT[:, dt, 0:NC_], in0=xT[:, dt, 0:NC_],
                                                             in1=tmpA[:, 0:NC_], op=ALU.add), [B_tmpA, B_x[dt]], [B_x[dt]])
                barrier()

        xch = [chan(f"c_x{i}") for i in range(2)]
        ych = [chan(f"c_y{i}") for i in range(2)]
        sch = chan("c_st")

        def load_x(p, xsrc, row0):
            with ExitStack() as ls:
                xtm = [ls.enter_context(nc.sbuf_tensor(f"xtm{p}_{i}", [128, D], F32)) for i in range(2)]
                B_xtm = [Buf(), Buf()]
                ntt = 4 + (1 if p == 0 else 0)
                for tt in range(ntt):
                    i = tt % 2
                    if tt < 4:
                        rows = 128
                        dma(SP, xch[i], xtm[i][:], xsrc[row0 + tt * 128:row0 + (tt + 1) * 128, :], writes=[B_xtm[i]])
                    else:
                        rows = NS
                        dma(SP, xch[i], xtm[i][0:NS, :], xs, writes=[B_xtm[i]])
                    for g4 in range(4):
                        pb = pbank()
                        for j in range(4):
                            dt = g4 * 4 + j
                            tr(pb, pb.t[:, j * 128:j * 128 + rows], xtm[i][0:rows, dt * 128:(dt + 1) * 128],
                               ident[0:rows, 0:rows], [B_xtm[i], B_ident], signal=(j == 3))
                        c0 = tt * 128
                        src = pb.t[:, 0:512].rearrange("p (j r) -> p j r", j=4)[:, :, 0:rows]
                        copy(ACT if g4 % 2 else DVE, xT[:, g4 * 4:(g4 + 1) * 4, c0:c0 + rows], src,
                             [pb], [B_x[g4 * 4 + j] for j in range(4)])
                barrier()

        def store_y(p, row0):
            with ExitStack() as ls:
                ytm = [ls.enter_context(nc.sbuf_tensor(f"ytm{p}_{i}", [128, D], F32)) for i in range(2)]
                B_ytm = [Buf(), Buf()]
                ntt = 4 + (1 if p == 0 else 0)
                for tt in range(ntt):
                    i = tt % 2
                    rows = 128 if tt < 4 else NS
                    c0 = tt * 128
                    for g4 in range(4):
                        pb = pbank()
                        for j in range(4):
                            dt = g4 * 4 + j
                            tr(pb, pb.t[0:rows, j * 128:(j + 1) * 128], xT[:, dt, c0:c0 + rows], ident[:],
                               [B_x[dt], B_ident], signal=(j == 3))
                        copy(ACT if g4 % 2 else DVE, ytm[i][0:rows, g4 * 512:(g4 + 1) * 512], pb.t[0:rows, :],
                             [pb], [B_ytm[i]])
                    if tt < 4:
                        dma(SP, ych[i], yp[row0 + tt * 128:row0 + (tt + 1) * 128, :], ytm[i][:], reads=[B_ytm[i]])
                    else:
                        dma(SP, ych[i], ys, ytm[i][0:NS, :], reads=[B_ytm[i]])
                barrier()

        def tt(out, a, b, op, reads, writes):
            return run(DVE, lambda: nc.vector.tensor_tensor(out=out, in0=a, in1=b, op=op), reads, writes)

        def ts(out, a, s1, s2, op0, op1, reads, writes):
            if s2 is None:
                return run(DVE, lambda: nc.vector.tensor_scalar(out=out, in0=a, scalar1=s1, scalar2=None, op0=op0),
                           reads, writes)
            return run(DVE, lambda: nc.vector.tensor_scalar(out=out, in0=a, scalar1=s1, scalar2=s2, op0=op0, op1=op1),
                       reads, writes)

        def stt(out, a, scalar, b, op0, op1, reads, writes):
            return run(DVE, lambda: nc.vector.scalar_tensor_tensor(out=out, in0=a, scalar=scalar, in1=b, op0=op0,
                                                                   op1=op1), reads, writes)

        def act(out, in_, func, reads, writes, bias=0.0, scale=1.0):
            return run(ACT, lambda: nc.scalar.activation(out=out, in_=in_, func=func, bias=bias, scale=scale),
                       reads, writes)

        MUL, ADD, SUB = ALU.mult, ALU.add, ALU.subtract

        def ttE(E, out, a, b, op, reads, writes):
            return run(E, lambda: E.e.tensor_tensor(out=out, in0=a, in1=b, op=op), reads, writes)

        def tsE(E, out, a, s1, s2, op0, op1, reads, writes):
            if s2 is None:
                return run(E, lambda: E.e.tensor_scalar(out=out, in0=a, scalar1=s1, scalar2=None, op0=op0), reads, writes)
            return run(E, lambda: E.e.tensor_scalar(out=out, in0=a, scalar1=s1, scalar2=s2, op0=op0, op1=op1), reads, writes)

        sp = sb("s5par", [128, 24, 32]); B_sp = Buf()
        (I_DT, I_AR, I_TH, I_MOD, I_CT, I_ST, I_ARE, I_AIM, I_NAIM, I_CR, I_CI, I_ICR, I_ICI, I_NICI,
         I_E128R, I_E128I, I_NE128I, I_T0, I_T1, I_T2, I_T3, I_NCI, I_T4, I_T5) = range(24)

        def P(i):
            return sp[:, i, :]
        spi = sb("s5pari", [128, 32], I32)
        wkr = sb("wkr", [128, 8, 32]); wki = sb("wki", [128, 8, 32])
        akr = sb("akr", [128, 8, 32]); aki = sb("aki", [128, 8, 32]); naki = sb("naki", [128, 8, 32])
        apr = sb("apr", [128, 4, 32]); api = sb("api", [128, 4, 32])
        e127 = sb("e127", [128, 3, 32])
        HPr = sb("HPr", [128, 32, 4]); HPi = sb("HPi", [128, 32, 4]); B_HP = Buf()
        rl_re = sb("rl_re", [128, 32]); rl_im = sb("rl_im", [128, 32]); B_rl = Buf()
        hgp = sb("hgp", [128, 16]); B_hgp = Buf()
        Shg = sb("Shg", [128, 8, 128]); B_S = [Buf() for _ in range(8)]
        R_, W_ = [B_sp, B_lam], [B_sp]

        def cmul(o_re, o_im, a_re, a_im, b_re, b_im, t0, t1):
            tt(t0, a_re, b_re, MUL, R_, W_); tt(t1, a_im, b_im, MUL, R_, W_)
            tt(o_re, t0, t1, SUB, R_, W_)
            tt(t0, a_re, b_im, MUL, R_, W_); tt(t1, a_im, b_re, MUL, R_, W_)
            tt(o_im, t0, t1, ADD, R_, W_)

        def prep_params():
            lre, lim, ldt = lam[:, 0:32], lam[:, 32:64], lam[:, 64:96]
            act(P(I_DT), ldt, AF.Exp, R_, W_)
            tt(P(I_AR), lre, P(I_DT), MUL, R_, W_)
            tt(P(I_TH), lim, P(I_DT), MUL, R_, W_)
            act(P(I_MOD), P(I_AR), AF.Exp, R_, W_)
            ts(P(I_T0), P(I_TH), 1.0 / (2 * math.pi), None, MUL, None, R_, W_)
            run(DVE, lambda: nc.vector.tensor_copy(out=spi[:], in_=P(I_T0)), R_, W_)
            run(DVE, lambda: nc.vector.tensor_copy(out=P(I_T0), in_=spi[:]), R_, W_)
            stt(P(I_T1), P(I_T0), -2 * math.pi, P(I_TH), MUL, ADD, R_, W_)
            act(P(I_ST), P(I_T1), AF.Sin, R_, W_)
            act(P(I_T2), P(I_T1), AF.Sin, R_, W_, scale=0.5)
            tt(P(I_T2), P(I_T2), P(I_T2), MUL, R_, W_)
            ts(P(I_CT), P(I_T2), -2.0, 1.0, MUL, ADD, R_, W_)
            tt(P(I_ARE), P(I_MOD), P(I_CT), MUL, R_, W_)
            tt(P(I_AIM), P(I_MOD), P(I_ST), MUL, R_, W_)
            ts(P(I_NAIM), P(I_AIM), -1.0, None, MUL, None, R_, W_)
            ts(P(I_T0), P(I_ARE), -1.0, None, ADD, None, R_, W_)
            ts(P(I_T1), lim, -1.0, None, MUL, None, R_, W_)
            cmul(P(I_CR), P(I_CI), P(I_T0), P(I_AIM), lre, P(I_T1), P(I_T2), P(I_T3))
            tt(P(I_T2), lre, lre, MUL, R_, W_); tt(P(I_T3), lim, lim, MUL, R_, W_)
            tt(P(I_T2), P(I_T2), P(I_T3), ADD, R_, W_)
            run(DVE, lambda: nc.vector.reciprocal(out=P(I_T2), in_=P(I_T2)), R_, W_)
            tt(P(I_CR), P(I_CR), P(I_T2), MUL, R_, W_); tt(P(I_CI), P(I_CI), P(I_T2), MUL, R_, W_)
            ts(P(I_NCI), P(I_CI), -1.0, None, MUL, None, R_, W_)
            tt(P(I_T2), P(I_CR), P(I_CR), MUL, R_, W_); tt(P(I_T3), P(I_CI), P(I_CI), MUL, R_, W_)
            tt(P(I_T2), P(I_T2), P(I_T3), ADD, R_, W_)
            run(DVE, lambda: nc.vector.reciprocal(out=P(I_T2), in_=P(I_T2)), R_, W_)
            tt(P(I_ICR), P(I_CR), P(I_T2), MUL, R_, W_)
            tt(P(I_NICI), P(I_CI), P(I_T2), MUL, R_, W_)
            ts(P(I_ICI), P(I_NICI), -1.0, None, MUL, None, R_, W_)
            copy(DVE, wkr[:, 0, :], P(I_CT), R_, W_); copy(DVE, wki[:, 0, :], P(I_ST), R_, W_)
            for k in range(1, 8):
                cmul(wkr[:, k, :], wki[:, k, :], wkr[:, k - 1, :], wki[:, k - 1, :], wkr[:, k - 1, :], wki[:, k - 1, :],
                     P(I_T2), P(I_T3))
            copy(DVE, P(I_E128R), wkr[:, 7, :], R_, W_); copy(DVE, P(I_E128I), wki[:, 7, :], R_, W_)
            ts(P(I_NE128I), P(I_E128I), -1.0, None, MUL, None, R_, W_)
            copy(DVE, akr[:, 0, :], P(I_ARE), R_, W_); copy(DVE, aki[:, 0, :], P(I_AIM), R_, W_)
            for k in range(1, 8):
                cmul(akr[:, k, :], aki[:, k, :], akr[:, k - 1, :], aki[:, k - 1, :], akr[:, k - 1, :], aki[:, k - 1, :],
                     P(I_T2), P(I_T3))
            ts(naki[:], aki[:], -1.0, None, MUL, None, R_, W_)
            copy(DVE, apr[:, 0, :], akr[:, 7, :], R_, W_); copy(DVE, api[:, 0, :], aki[:, 7, :], R_, W_)
            cmul(apr[:, 1, :], api[:, 1, :], apr[:, 0, :], api[:, 0, :], apr[:, 0, :], api[:, 0, :], P(I_T2), P(I_T3))
            cmul(apr[:, 2, :], api[:, 2, :], apr[:, 1, :], api[:, 1, :], apr[:, 0, :], api[:, 0, :], P(I_T2), P(I_T3))
            cmul(apr[:, 3, :], api[:, 3, :], apr[:, 1, :], api[:, 1, :], apr[:, 1, :], api[:, 1, :], P(I_T2), P(I_T3))
            ts(P(I_T4), wki[:, 0, :], -1.0, None, MUL, None, R_, W_)
            cmul(e127[:, 0, :], e127[:, 1, :], wkr[:, 7, :], wki[:, 7, :], wkr[:, 0, :], P(I_T4), P(I_T2), P(I_T3))
            ts(e127[:, 2, :], e127[:, 1, :], -1.0, None, MUL, None, R_, W_)
            run(DVE, lambda: nc.vector.memset(rl_re[:], 0.0), [], [B_rl])
            run(DVE, lambda: nc.vector.memset(rl_im[:], 0.0), [], [B_rl])
            tt(hgp[:, 8:16], misc[:, 8:16], misc[:, 16:24], SUB, [B_misc], [B_hgp])
            act(hgp[:, 0:8], hgp[:, 8:16], AF.Sigmoid, [B_hgp], [B_hgp])
            ts(hgp[:, 8:16], hgp[:, 0:8], -1.0, 1.0, MUL, ADD, [B_hgp], [B_hgp])
            for hd in range(8):
                run(DVE, lambda: nc.vector.memset(Shg[:, hd, :], 0.0), [], [B_S[hd]])

        mch = [chan(f"c_m{i}") for i in range(6)]

        def mixer(p, last, so=False):
            NC_ = ncols(p)
            blks = blocks(p)
            hr = lambda k: (hT[:, k, :], B_h[k])
            with ExitStack() as ms:
                def msb(name, shape, dt=F32):
                    return ms.enter_context(nc.sbuf_tensor(f"m{p}_{name}", list(shape), dt))
                s5o = msb("s5o", [128, 8, NT], BF16); B_s5o = [Buf() for _ in range(8)]
                prenorm(p, 2)
                with ExitStack() as ss:
                    def ssb(name, shape, dt=F32):
                        return ss.enter_context(nc.sbuf_tensor(f"s{p}_{name}", list(shape), dt))
                    ct = ssb("ct", [128, 32, 128]); st = ssb("st", [128, 32, 128]); B_tab = Buf()
                    scr = ssb("scr", [128, 32 * 64]); B_scr = Buf()
                    xrs = [ssb(f"xr{i}", [128, 2 * TB]) for i in range(2)]
                    xis = [ssb(f"xi{i}", [128, 2 * TB]) for i in range(2)]
                    B_xs_ = [Buf(), Buf()]
                    wa, wb = scr[:, 0:1024], scr[:, 1024:2048]
                    pla = ssb("pla", [128, TB]); plb = ssb("plb", [128, TB]); B_pl = Buf()
                    hre = ssb("hre", [128, 2, TB], BF16); him = ssb("him", [128, 2, TB], BF16); B_hc = Buf()
                    ini = ssb("ini", [128, 6, 2]); B_ini = Buf()
                    xs_re = ssb("xs_re", [128, 2, 4, NS]); xs_im = ssb("xs_im", [128, 2, 4, NS]); B_xsm = [Buf(), Buf()]
                    hsb_re = ssb("hsb_re", [128, 4, NS], BF16); hsb_im = ssb("hsb_im", [128, 4, NS], BF16); B_hsb = Buf()
                    u_f = ssb("u_f", [128, NT]); u_b = ssb("u_b", [128, NT], BF16); B_u = Buf()
                    ufs = [u_f, rstd]; B_uf = [Buf(), B_rstd]
                    bb = [ssb(f"bb{i}", [128, 1024], BF16) for i in range(2)]; B_bb = [Buf(), Buf()]
                    cf1 = ssb("cf", [128, 4, 256]); B_cf1 = Buf()
                    cb_re = ssb("cb_re", [128, 4, 128], BF16); cb_im = ssb("cb_im", [128, 4, 128], BF16)
                    cbr = [cb_re[:], sqt[:, 0:512].rearrange("p (j c) -> p j c", j=4)]
                    cbi = [cb_im[:], sqt2[:, 0:512].rearrange("p (j c) -> p j c", j=4)]
                    B_cb = [Buf(), Buf()]
                    zT = ssb("zT", [128, 8, NT], BF16); B_z = [Buf() for _ in range(8)]
                    yv, B_yv = tmpB, B_tmpB
                    TR, TW = [B_tab, B_sp, B_scr], [B_tab]
                    if so:
                        run(DVE, lambda: nc.vector.memset(ct[:, :, 127:128], 1.0), [], TW)
                        run(DVE, lambda: nc.vector.memset(st[:, :, 127:128], 0.0), [], TW)
                        copy(DVE, ct[:, :, 126:127], P(I_ARE).unsqueeze(2), TR, TW)
                        copy(DVE, st[:, :, 126:127], P(I_NAIM).unsqueeze(2), TR, TW)
                        for k in range(1, 7):
                            n = 1 << k
                            wr_b = akr[:, k, :].unsqueeze(2).to_broadcast([128, 32, n])
                            wi_b = naki[:, k, :].unsqueeze(2).to_broadcast([128, 32, n])
                            sc3 = scr[:, 0:32 * n].rearrange("p (a b) -> p a b", a=32)
                            src_c, src_s = ct[:, :, 128 - n:128], st[:, :, 128 - n:128]
                            dst_c, dst_s = ct[:, :, 128 - 2 * n:128 - n], st[:, :, 128 - 2 * n:128 - n]
                            tt(dst_c, src_c, wr_b, MUL, TR, TW)
                            tt(sc3, src_s, wi_b, MUL, TR, [B_scr])
                            tt(dst_c, dst_c, sc3, SUB, TR, TW)
                            tt(dst_s, src_s, wr_b, MUL, TR, TW)
                            tt(sc3, src_c, wi_b, MUL, TR, [B_scr])
                            tt(dst_s, dst_s, sc3, ADD, TR, TW)
                    else:
                        run(DVE, lambda: nc.vector.memset(ct[:, :, 0:1], 1.0), [], TW)
                        run(DVE, lambda: nc.vector.memset(st[:, :, 0:1], 0.0), [], TW)
                        copy(DVE, ct[:, :, 1:2], P(I_CT).unsqueeze(2), TR, TW)
                        copy(DVE, st[:, :, 1:2], P(I_ST).unsqueeze(2), TR, TW)
                        for k in range(1, 7):
                            n = 1 << k
                            wr_b = wkr[:, k, :].unsqueeze(2).to_broadcast([128, 32, n])
                            wi_b = wki[:, k, :].unsqueeze(2).to_broadcast([128, 32, n])
                            sc3 = scr[:, 0:32 * n].rearrange("p (a b) -> p a b", a=32)
                            tt(ct[:, :, n:2 * n], ct[:, :, 0:n], wr_b, MUL, TR, TW)
                            tt(sc3, st[:, :, 0:n], wi_b, MUL, TR, [B_scr])
                            tt(ct[:, :, n:2 * n], ct[:, :, n:2 * n], sc3, SUB, TR, TW)
                            tt(st[:, :, n:2 * n], st[:, :, 0:n], wr_b, MUL, TR, TW)
                            tt(sc3, ct[:, :, 0:n], wi_b, MUL, TR, [B_scr])
                            tt(st[:, :, n:2 * n], st[:, :, n:2 * n], sc3, ADD, TR, TW)
                    if p == 0:
                        h0re = ssb("h0re", [128, 32, NS]); h0im = ssb("h0im", [128, 32, NS]); B_h0 = Buf()
                        hsre, hsim, B_hs = h0re, h0im, B_h0
                        stg = [pla[0:NS, :], plb[0:NS, :]]; B_stg = [B_pl, B_pl]
                        n_ld = 0
                        for (src_d, dst) in ((sre, h0re), (sim, h0im)):
                            for c8 in range(8):
                                i = n_ld % 2; n_ld += 1
                                dma(SP, mch[i], stg[i][:], src_d[:, c8 * 512:(c8 + 1) * 512], writes=[B_stg[i]])
                                pb = pbank()
                                for j in range(4):
                                    tr(pb, pb.t[:, j * NS:(j + 1) * NS], stg[i][:, j * 128:(j + 1) * 128],
                                       ident[0:NS, 0:NS], [B_stg[i], B_ident], signal=(j == 3))
                                copy(DVE, dst[:, c8 * 4:(c8 + 1) * 4, :],
                                     pb.t[:, 0:4 * NS].rearrange("p (a b) -> p a b", a=4), [pb], [B_h0])
                        icr_b = P(I_ICR).unsqueeze(2).to_broadcast([128, 32, NS])
                        ici_b = P(I_ICI).unsqueeze(2).to_broadcast([128, 32, NS])
                        s3 = scr[:, 0:32 * NS].rearrange("p (a b) -> p a b", a=32)
                        s3b = scr[:, 1024:1024 + 32 * NS].rearrange("p (a b) -> p a b", a=32)
                        HR = [B_h0, B_sp, B_scr]
                        s3c = scr[:, 512:512 + 32 * NS].rearrange("p (a b) -> p a b", a=32)
                        tt(s3, h0re[:], icr_b, MUL, HR, [B_scr]); tt(s3b, h0im[:], ici_b, MUL, HR, [B_scr])
                        tt(s3c, s3, s3b, SUB, HR, [B_scr])
                        tt(s3, h0re[:], ici_b, MUL, HR, [B_scr]); tt(s3b, h0im[:], icr_b, MUL, HR, [B_scr])
                        tt(h0im[:], s3, s3b, ADD, HR, [B_h0])
                        copy(DVE, h0re[:], s3c, [B_scr], [B_h0])
                    def prep_gt(gt):
                        ib = gt % 2
                        g4 = slice(gt * 4, gt * 4 + 4)
                        dma(POOL, mch[2 + ib], bb[ib][:], bblk_d[gt], writes=[B_bb[ib]])
                        wb_, wv_ = wload(win, 0, 16, gt * 128, 128)
                        if not so:
                            dma(SP, mch[4], cf1[:], cblk_d[gt].rearrange("j p c -> p j c"), writes=[B_cf1])
                            cr_, ci_ = cf1[:, :, 0:128], cf1[:, :, 128:256]
                            crb = P(I_CR)[:, g4].unsqueeze(2).to_broadcast([128, 4, 128])
                            cib = P(I_CI)[:, g4].unsqueeze(2).to_broadcast([128, 4, 128])
                            wa3 = pla[:].rearrange("p (j c) -> p j c", j=4)
                            wb3 = plb[:].rearrange("p (j c) -> p j c", j=4)
                            CR = [B_cf1, B_sp, B_pl]
                            ttE(DVE, wa3, cr_, crb, MUL, CR, [B_pl]); ttE(DVE, wb3, ci_, cib, MUL, CR, [B_pl])
                            ttE(DVE, cbr[ib], wa3, wb3, SUB, CR, [B_cb[ib]])
                            ttE(DVE, wa3, cr_, cib, MUL, CR, [B_pl]); ttE(DVE, wb3, ci_, crb, MUL, CR, [B_pl])
                            ttE(DVE, wa3, wa3, wb3, ADD, CR, [B_pl])
                            tsE(DVE, cbi[ib], wa3, -1.0, None, MUL, None, CR, [B_cb[ib]])

                        def cons_u(outs):
                            for (c0, n, pb, ap) in outs:
                                copy(ACT, ufs[ib][:, c0:c0 + n], ap, [pb], [B_uf[ib]])
                                copy(ACT, u_b[:, c0:c0 + n], ap, [pb], [B_u])
                        dense_tile(p, [(wb_, wv_, 0)], hr, cons_u)

                    def produce_X(gt, half, s_):
                        ib = gt % 2
                        for jj in range(2):
                            j = half * 2 + jj
                            for (c0, n) in blks:
                                for comp in range(2):
                                    if c0 == 0:
                                        pb = pbank()
                                    else:
                                        pb, _ = sslot()
                                    oap = pb.t[:, 0:n]
                                    mm(pb, oap, [(bb[ib][:, comp * 512 + j * 128:comp * 512 + (j + 1) * 128],
                                                  u_b[:, c0:c0 + n])], [B_bb[ib], B_u])
                                    if c0 == 0:
                                        dst = (xrs[s_] if comp == 0 else xis[s_])[:, jj * TB:(jj + 1) * TB]
                                        copy(ACT, dst, oap, [pb], [B_xs_[s_]])
                                    else:
                                        dst = (xs_re if comp == 0 else xs_im)[:, ib, j, :]
                                        copy(ACT, dst, oap, [pb], [B_xsm[ib]])

                    def consume(gt, half, s_):
                        ib = gt % 2
                        tl0 = gt * 4 + half * 2
                        t2 = slice(tl0, tl0 + 2)
                        xr, xi, B_xr = xrs[s_], xis[s_], B_xs_[s_]
                        x4r = xr[:].rearrange("p (j k t) -> p j k t", j=2, k=4)
                        x4i = xi[:].rearrange("p (j k t) -> p j k t", j=2, k=4)
                        a4 = wa[:].rearrange("p (j k t) -> p j k t", j=2, k=4)
                        b4 = wb[:].rearrange("p (j k t) -> p j k t", j=2, k=4)
                        ctb = ct[:, t2, :].unsqueeze(2).to_broadcast([128, 2, 4, 128])
                        stb = st[:, t2, :].unsqueeze(2).to_broadcast([128, 2, 4, 128])
                        XR = [B_xr, B_tab, B_scr]
                        tt(a4, x4r, ctb, MUL, XR, [B_scr]); tt(b4, x4i, stb, MUL, XR, [B_scr])
                        tt(a4, a4, b4, ADD, XR, [B_scr])
                        tt(b4, x4i, ctb, MUL, XR, [B_scr]); tt(x4r, x4r, stb, MUL, XR, [B_xr])
                        tt(b4, b4, x4r, SUB, XR, [B_scr])
                        if so:
                            run(DVE, lambda: nc.vector.tensor_reduce(out=HPr[:, t2, :], in_=a4, axis=mybir.AxisListType.X,
                                                                     op=ADD), [B_scr], [B_HP])
                            run(DVE, lambda: nc.vector.tensor_reduce(out=HPi[:, t2, :], in_=b4, axis=mybir.AxisListType.X,
                                                                     op=ADD), [B_scr], [B_HP])
                            return
                        for k in range(4):
                            if k == 0:
                                lr, li, LB = rl_re[:, t2], rl_im[:, t2], B_rl
                            else:
                                lr, li, LB = x4i[:, :, k - 1, 127], x4r[:, :, k - 1, 127], B_xr
                            IR = [LB, B_sp, B_ini]
                            er, ei = P(I_E128R)[:, t2], P(I_E128I)[:, t2]
                            tt(ini[:, 0, :], lr, er, MUL, IR, [B_ini]); tt(ini[:, 1, :], li, ei, MUL, IR, [B_ini])
                            tt(ini[:, 2, :], li, er, MUL, IR, [B_ini]); tt(ini[:, 3, :], lr, ei, MUL, IR, [B_ini])
                            tt(ini[:, 4, :], ini[:, 0, :], ini[:, 1, :], SUB, IR, [B_ini])
                            tt(ini[:, 5, :], ini[:, 2, :], ini[:, 3, :], ADD, IR, [B_ini])
                            for jj in range(2):
                                mod_b = P(I_MOD)[:, tl0 + jj:tl0 + jj + 1].to_broadcast([128, 128])
                                run(DVE, lambda: nc.vector.tensor_tensor_scan(
                                    out=x4i[:, jj, k, :], data0=mod_b, data1=a4[:, jj, k, :], initial=ini[:, 4, jj:jj + 1],
                                    op0=MUL, op1=ADD), [B_scr, B_ini, B_sp], [B_xr])
                                run(DVE, lambda: nc.vector.tensor_tensor_scan(
                                    out=x4r[:, jj, k, :], data0=mod_b, data1=b4[:, jj, k, :], initial=ini[:, 5, jj:jj + 1],
                                    op0=MUL, op1=ADD), [B_scr, B_ini, B_sp], [B_xr])
                        copy(DVE, rl_re[:, t2], x4i[:, :, 3, 127], [B_xr], [B_rl])
                        copy(DVE, rl_im[:, t2], x4r[:, :, 3, 127], [B_xr], [B_rl])
                        pa3 = pla[:].rearrange("p (k t) -> p k t", k=4)
                        pb3 = plb[:].rearrange("p (k t) -> p k t", k=4)
                        PR = [B_xr, B_tab, B_pl]
                        for jj in range(2):
                            ctj = ct[:, tl0 + jj, :].unsqueeze(1).to_broadcast([128, 4, 128])
                            stj = st[:, tl0 + jj, :].unsqueeze(1).to_broadcast([128, 4, 128])
                            rre, rim = x4i[:, jj, :, :], x4r[:, jj, :, :]
                            h3r = hre[:, jj, :].rearrange("p (k t) -> p k t", k=4)
                            h3i = him[:, jj, :].rearrange("p (k t) -> p k t", k=4)
                            G_ = nc.gpsimd if POOL_POST else nc.vector
                            PE_ = POOL if POOL_POST else DVE
                            run(PE_, lambda: G_.tensor_tensor(out=pa3, in0=rre, in1=ctj, op=MUL), PR, [B_pl])
                            run(PE_, lambda: G_.tensor_tensor(out=pb3, in0=rim, in1=stj, op=MUL), PR, [B_pl])
                            run(PE_, lambda: G_.tensor_tensor(out=h3r, in0=pa3, in1=pb3, op=SUB), [B_pl], [B_hc])
                            run(PE_, lambda: G_.tensor_tensor(out=pa3, in0=rre, in1=stj, op=MUL), PR, [B_pl])
                            run(PE_, lambda: G_.tensor_tensor(out=pb3, in0=rim, in1=ctj, op=MUL), PR, [B_pl])
                            run(PE_, lambda: G_.tensor_tensor(out=h3i, in0=pa3, in1=pb3, op=ADD), [B_pl], [B_hc])
                        YB = LBANK if ib == 0 else LBANK2
                        for jj in range(2):
                            j = half * 2 + jj
                            mm(YB, YB.t[:, 0:TB], [(cbr[ib][:, j, :], hre[:, jj, :]), (cbi[ib][:, j, :], him[:, jj, :])],
                               [B_cb[ib], B_hc], start=(j == 0), stop=(j == 3))

                    def epilogue(gt):
                        ib = gt % 2
                        g4 = slice(gt * 4, gt * 4 + 4)
                        if p == 0:
                            h0r, h0i = h0re[:, g4, :], h0im[:, g4, :]
                            areb = P(I_ARE)[:, g4].unsqueeze(2).to_broadcast([128, 4, NS])
                            aimb = P(I_AIM)[:, g4].unsqueeze(2).to_broadcast([128, 4, NS])
                            sA = wa[:, 0:4 * NS].rearrange("p (j c) -> p j c", j=4)
                            sB = wa[:, 64:64 + 4 * NS].rearrange("p (j c) -> p j c", j=4)
                            sC = wa[:, 128:128 + 4 * NS].rearrange("p (j c) -> p j c", j=4)
                            sD = wa[:, 192:192 + 4 * NS].rearrange("p (j c) -> p j c", j=4)
                            SR = [B_h0, B_sp, B_scr, B_xsm[ib]]
                            tt(sA, h0r, areb, MUL, SR, [B_scr]); tt(sB, h0i, aimb, MUL, SR, [B_scr])
                            tt(sC, h0i, areb, MUL, SR, [B_scr]); tt(sD, h0r, aimb, MUL, SR, [B_scr])
                            tt(sA, sA, sB, SUB, SR, [B_scr]); tt(sC, sC, sD, ADD, SR, [B_scr])
                            tt(h0r, sA, xs_re[:, ib], ADD, SR, [B_h0]); tt(h0i, sC, xs_im[:, ib], ADD, SR, [B_h0])
                            copy(DVE, hsb_re[:], h0r, [B_h0], [B_hsb]); copy(DVE, hsb_im[:], h0i, [B_h0], [B_hsb])
                            ysb, _ = sslot()
                            for j in range(4):
                                mm(ysb, ysb.t[:, 0:NS], [(cbr[ib][:, j, :], hsb_re[:, j, :]), (cbi[ib][:, j, :], hsb_im[:, j, :])],
                                   [B_cb[ib], B_hsb], start=(j == 0), stop=(j == 3))
                        for (c0, n) in blks:
                            yb = (LBANK if ib == 0 else LBANK2) if c0 == 0 else ysb
                            stt(yv[:, c0:c0 + n], ufs[ib][:, c0:c0 + n], misc[:, 24 + gt:25 + gt], yb.t[:, 0:n], MUL, ADD,
                                [B_uf[ib], B_misc, yb], [B_yv])
                        yy = yv[:, 0:NC_]; ta = tmpA[:, 0:NC_]
                        ttE(DVE, ta, yy, yy, MUL, [B_yv], [B_tmpA])
                        tsE(DVE, ta, ta, 0.044715, 1.0, MUL, ADD, [B_tmpA], [B_tmpA])
                        ttE(DVE, ta, ta, yy, MUL, [B_yv, B_tmpA], [B_tmpA])
                        act(ta, ta, AF.Sigmoid, [B_tmpA], [B_tmpA], scale=1.5957691216057308)
                        ttE(DVE, zT[:, gt, 0:NC_], yy, ta, MUL, [B_yv, B_tmpA], [B_z[gt]])

                    halves = [(g_, h_) for g_ in range(8) for h_ in range(2)]
                    prep_gt(0)
                    produce_X(0, 0, 0)
                    for idx, (gt, half) in enumerate(halves):
                        if idx + 1 < len(halves):
                            ng, nh = halves[idx + 1]
                            if nh == 0:
                                prep_gt(ng)
                            produce_X(ng, nh, (idx + 1) % 2)
                        consume(gt, half, idx % 2)
                        if not so and half == 0 and gt > 0:
                            epilogue(gt - 1)
                    if not so:
                        epilogue(7)
                    if so:
                        SW = [B_sp, B_rl, B_HP, B_scr]
                        q = [scr[:, i * 32:(i + 1) * 32] for i in range(8)]

                        def cm(o_r, o_i, a_r, a_i, b_r, b_i):
                            tt(q[4], a_r, b_r, MUL, SW, [B_scr]); tt(q[5], a_i, b_i, MUL, SW, [B_scr])
                            tt(q[6], a_r, b_i, MUL, SW, [B_scr]); tt(q[7], a_i, b_r, MUL, SW, [B_scr])
                            tt(o_r, q[4], q[5], SUB, SW, [B_scr]); tt(o_i, q[6], q[7], ADD, SW, [B_scr])
                        cm(q[0], q[1], rl_re[:], rl_im[:], e127[:, 0, :], e127[:, 1, :])
                        cm(q[2], q[3], q[0], q[1], apr[:, 3, :], api[:, 3, :])
                        for k in range(3):
                            cm(q[0], q[1], HPr[:, :, k], HPi[:, :, k], apr[:, 2 - k, :], api[:, 2 - k, :])
                            tt(q[2], q[2], q[0], ADD, SW, [B_scr]); tt(q[3], q[3], q[1], ADD, SW, [B_scr])
                        tt(q[2], q[2], HPr[:, :, 3], ADD, SW, [B_scr]); tt(q[3], q[3], HPi[:, :, 3], ADD, SW, [B_scr])
                        tt(q[4], q[2], e127[:, 0, :], MUL, SW, [B_scr]); tt(q[5], q[3], e127[:, 2, :], MUL, SW, [B_scr])
                        tt(q[6], q[2], e127[:, 2, :], MUL, SW, [B_scr]); tt(q[7], q[3], e127[:, 0, :], MUL, SW, [B_scr])
                        tt(rl_re[:], q[4], q[5], SUB, SW, [B_rl]); tt(rl_im[:], q[6], q[7], ADD, SW, [B_rl])
                    zr = lambda k: (zT[:, k, :], B_z[k])
                    for mt in (range(8) if not so else []):
                        wb_, wv_ = wload(wglu, 0, 8, mt * 128, 128)

                        def cons_g(outs, mt=mt):
                            for (c0, n, pb, ap) in outs:
                                act(tmpA[:, c0:c0 + n], ap, AF.Sigmoid, [pb, B_misc], [B_tmpA],
                                    bias=misc[:, 32 + mt:33 + mt])
                                tt(s5o[:, mt, c0:c0 + n], zT[:, mt, c0:c0 + n], tmpA[:, c0:c0 + n], MUL,
                                   [B_z[mt], B_tmpA], [B_s5o[mt]])
                        dense_tile(p, [(wb_, wv_, 0)], zr, cons_g)
                    if DEBUG and p == 0:
                        dma(POOL, sch, dbg_hg, zT[:], reads=B_z)
                    if last:
                        E127r, E127i = ct[:, :, 127], st[:, :, 127]
                        FR = [B_rl, B_tab, B_sp, B_scr]
                        a0, a1, a2, a3 = (scr[:, 0:32], scr[:, 32:64], scr[:, 64:96], scr[:, 96:128])
                        tt(a0, rl_re[:], E127r, MUL, FR, [B_scr]); tt(a1, rl_im[:], E127i, MUL, FR, [B_scr])
                        tt(a2, a0, a1, SUB, FR, [B_scr])
                        tt(a0, rl_re[:], E127i, MUL, FR, [B_scr]); tt(a1, rl_im[:], E127r, MUL, FR, [B_scr])
                        tt(a3, a0, a1, ADD, FR, [B_scr])
                        o0, o1 = scr[:, 128:160], scr[:, 160:192]
                        tt(a0, a2, P(I_CR), MUL, FR, [B_scr]); tt(a1, a3, P(I_CI), MUL, FR, [B_scr])
                        tt(o0, a0, a1, SUB, FR, [B_scr])
                        tt(a0, a2, P(I_CI), MUL, FR, [B_scr]); tt(a1, a3, P(I_CR), MUL, FR, [B_scr])
                        tt(o1, a0, a1, ADD, FR, [B_scr])
                        dma(SP, sch, o_re_p.rearrange("t p -> p t"), o0, reads=[B_scr], allow_slow_non_contiguous=True)
                        dma(SP, sch, o_im_p.rearrange("t p -> p t"), o1, reads=[B_scr], allow_slow_non_contiguous=True)
                    if p == 0:
                        cr_b = P(I_CR).unsqueeze(2).to_broadcast([128, 32, NS])
                        ci_b = P(I_CI).unsqueeze(2).to_broadcast([128, 32, NS])
                        HR = [B_hs, B_sp, B_scr, B_h0]
                        tt(s3, hsre[:], cr_b, MUL, HR, [B_scr]); tt(s3b, hsim[:], ci_b, MUL, HR, [B_scr])
                        tt(s3c, s3, s3b, SUB, HR, [B_scr])
                        tt(s3, hsre[:], ci_b, MUL, HR, [B_scr]); tt(s3b, hsim[:], cr_b, MUL, HR, [B_scr])
                        tt(h0im[:], s3, s3b, ADD, HR, [B_h0])
                        copy(DVE, h0re[:], s3c, [B_scr], [B_h0])
                        n_st = 0
                        for (src_t, dst_d) in ((h0re, o_re_s), (h0im, o_im_s)):
                            for c8 in range(8):
                                i = n_st % 2; n_st += 1
                                pb = pbank()
                                for j in range(4):
                                    tr(pb, pb.t[0:NS, j * 128:(j + 1) * 128], src_t[:, c8 * 4 + j, :], ident[:],
                                       [B_h0, B_ident], signal=(j == 3))
                                copy(ACT, stg[i][:], pb.t[0:NS, :], [pb], [B_stg[i]])
                                dma(SP, mch[i], dst_d[:, c8 * 512:(c8 + 1) * 512], stg[i][:], reads=[B_stg[i]])
                    barrier()
                hgT = msb("hgT", [128, 8, NT], BF16); B_hg = [Buf() for _ in range(8)]
                with ExitStack() as hs:
                    def hsb(name, shape, dt=F32):
                        return hs.enter_context(nc.sbuf_tensor(f"h{p}_{name}", list(shape), dt))
                    qs = hsb("qs", [128, NT]); B_qs = Buf()
                    ff = hsb("ff", [128, NT]); B_ff = Buf()
                    lf = hsb("lf", [128, NT]); B_lf = Buf()
                    kk = hsb("kk", [128, NT]); B_kk = Buf()
                    sog = hsb("sog", [128, NT]); B_sog = Buf()
                    G = hsb("G", [128, TB]); B_G = Buf()
                    Gp = hsb("Gp", [128, TB]); B_Gp = Buf()
                    eg = hsb("eg", [128, TB]); B_eg = Buf()
                    qt = hsb("qt", [128, TB], BF16); kt_ = hsb("kt", [128, TB], BF16); B_qk = Buf()
                    ecol = hsb("ecol", [128, 12]); B_ecol = Buf()
                    itm = hsb("itm", [128, 4, 128], BF16); B_itm = Buf()
                    i_s = hsb("i_s", [NS, 128]); B_is = Buf()
                    oT = hsb("oT", [128, NT]); B_oT = Buf()
                    attT = hsb("attT", [128, 128], BF16); B_att = Buf()
                    ktm = hsb("ktm", [128, 128], BF16); B_ktm = Buf()
                    Sb = hsb("Sb", [128, 128], BF16); B_Sb = Buf()
                    run(DVE, lambda: nc.vector.memset(attT[:], 0.0), [], [B_att])
                    if not so:
                        klo = hsb("klo", [128, TB], BF16); khi = hsb("khi", [128, TB], BF16); qhi = hsb("qhi", [128, TB], BF16)
                        B_kq2 = Buf()
                        for t_ in (klo, khi, qhi):
                            run(DVE, lambda: nc.vector.memset(t_[:], 0.0), [], [B_kq2])
                    if p == 0:
                        S0 = hsb("S0", [128, NS, 128]); B_S0 = Buf()
                        Sn = hsb("Sn", [128, NS, 128]); B_Sn = Buf()
                    for hd in range(8):
                        if not so:
                            bq, vq = wload(win, 0, 16, 1024 + hd * 128, 128)
                        bf_, vf = wload(win, 0, 16, 2048 + hd * 128, 128)
                        bi, vi_ = wload(win, 0, 16, 3072 + hd * 128, 128)
                        if not so:
                            bo, vo = wload(win, 0, 16, 4096 + hd * 128, 128)
                        lbc, omc = hgp[:, hd:hd + 1], hgp[:, 8 + hd:9 + hd]

                        def cons_q(outs):
                            for (c0, n, pb, ap) in outs:
                                act(qs[:, c0:c0 + n], ap, AF.Silu, [pb], [B_qs])

                        def cons_f(outs):
                            for (c0, n, pb, ap) in outs:
                                act(ff[:, c0:c0 + n], ap, AF.Sigmoid, [pb], [B_ff])
                        def cons_o(outs):
                            for (c0, n, pb, ap) in outs:
                                act(sog[:, c0:c0 + n], ap, AF.Silu, [pb], [B_sog])
                        if not so:
                            dense_tile(p, [(bq, vq, 0)], hr, cons_q)
                        dense_tile(p, [(bf_, vf, 0)], hr, cons_f)
                        if not so:
                            dense_tile(p, [(bo, vo, 0)], hr, cons_o)
                        ts(ff[:, 0:NC_], ff[:, 0:NC_], omc, lbc, MUL, ADD, [B_ff, B_hgp], [B_ff])
                        act(lf[:, 0:NC_], ff[:, 0:NC_], AF.Ln, [B_ff], [B_lf])
                        ts(kk[:, 0:NC_], ff[:, 0:NC_], -1.0, 1.0, MUL, ADD, [B_ff], [B_kk])
                        for tk in range(4):
                            pb = pbank()
                            mm(pb, pb.t[:, 0:128], [(hT[:, k, tk * 128:(tk + 1) * 128], vi_[:, k, :]) for k in range(16)],
                               [bi] + B_h)
                            copy(ACT, itm[:, tk, :], pb.t[:, 0:128], [pb], [B_itm])
                        if p == 0:
                            pb, _ = sslot()
                            mm(pb, pb.t[0:NS, 0:128], [(hT[:, k, TB:NT], vi_[:, k, :]) for k in range(16)], [bi] + B_h)
                            copy(ACT, i_s[:], pb.t[0:NS, 0:128], [pb], [B_is])
                        run(DVE, lambda: nc.vector.tensor_tensor_scan(out=G[:], data0=scanm[:], data1=lf[:, 0:TB],
                                                                      initial=0.0, op0=MUL, op1=ADD),
                            [B_lf, B_scanm], [B_G])
                        G3 = G[:].rearrange("p (k t) -> p k t", k=4)
                        Gp3 = Gp[:].rearrange("p (k t) -> p k t", k=4)
                        tt(Gp3, G3, G3[:, :, 63:64].to_broadcast([128, 4, 128]), SUB, [B_G], [B_Gp])
                        act(ecol[:, 0:4], G3[:, :, 127], AF.Exp, [B_G], [B_ecol])
                        act(ecol[:, 8:12], Gp3[:, :, 127], AF.Exp, [B_Gp], [B_ecol])
                        if not so:
                            act(ecol[:, 4:8], G3[:, :, 63], AF.Exp, [B_G], [B_ecol])
                            act(eg[:], Gp[:], AF.Exp, [B_Gp], [B_eg])
                            tt(qt[:], qs[:, 0:TB], eg[:], MUL, [B_qs, B_eg], [B_qk])
                            q3_ = qs[:, 0:TB].rearrange("p (k t) -> p k t", k=4); e3_ = eg[:].rearrange("p (k t) -> p k t", k=4)
                            tt(qhi[:].rearrange("p (k t) -> p k t", k=4)[:, :, 64:128], q3_[:, :, 64:128], e3_[:, :, 64:128], MUL,
                               [B_qs, B_eg], [B_kq2])
                        act(eg[:], Gp[:], AF.Exp, [B_Gp, B_qk], [B_eg], scale=-1.0)
                        tt(kt_[:], kk[:, 0:TB], eg[:], MUL, [B_kk, B_eg], [B_qk])
                        if not so:
                            k3_ = kk[:, 0:TB].rearrange("p (k t) -> p k t", k=4); e3_ = eg[:].rearrange("p (k t) -> p k t", k=4)
                            tt(klo[:].rearrange("p (k t) -> p k t", k=4)[:, :, 0:64], k3_[:, :, 0:64], e3_[:, :, 0:64], MUL,
                               [B_kk, B_eg], [B_kq2])
                            tt(khi[:].rearrange("p (k t) -> p k t", k=4)[:, :, 64:128], k3_[:, :, 64:128], e3_[:, :, 64:128], MUL,
                               [B_kk, B_eg], [B_kq2])
                        for k in range(4):
                            sl = slice(k * 128, (k + 1) * 128)
                            if not so:
                                pa = pbank()
                                mm(pa, pa.t[:, 0:128], [(klo[:, sl], qt[:, sl]), (khi[:, sl], qhi[:, sl])], [B_qk, B_kq2])
                                run(DVE, lambda: nc.vector.copy_predicated(out=attT[:], mask=mask[:].bitcast(I32), data=pa.t[:, 0:128]),
                                    [pa, B_mask], [B_att])
                                ts(Sb[:], Shg[:, hd, :], ecol[:, 4 + k:5 + k], None, MUL, None, [B_S[hd], B_ecol], [B_Sb])
                                po = pbank()
                                mm(po, po.t[:, 0:128], [(itm[:, k, :], attT[:]), (Sb[:], qt[:, sl])],
                                   [B_itm, B_att, B_Sb, B_qk])
                                copy(ACT, oT[:, sl], po.t[:, 0:128], [po], [B_oT])
                            pt = pbank()
                            ptb = pt.t[:].bitcast(BF16)
                            tr(pt, ptb[:, 0:128], kt_[:, sl], identb[:], [B_qk, B_identb])
                            copy(DVE, ktm[:], ptb[:, 0:128], [pt], [B_ktm])
                            pd = pbank()
                            mm(pd, pd.t[:, 0:128], [(ktm[:], itm[:, k, :])], [B_ktm, B_itm])
                            ts(Shg[:, hd, :], Shg[:, hd, :], ecol[:, k:k + 1], None, MUL, None, [B_S[hd], B_ecol], [B_S[hd]])
                            stt(Shg[:, hd, :], pd.t[:, 0:128], ecol[:, 8 + k:9 + k], Shg[:, hd, :], MUL, ADD,
                                [pd, B_ecol, B_S[hd]], [B_S[hd]])
                        if last:
                            dma(SP, sch, o_hg_p[hd], Shg[:, hd, :], reads=[B_S[hd]])
                        if p == 0:
                            dma(SP, mch[0], S0[:], shg[:, hd].rearrange("t k v -> k t v"), writes=[B_S0])
                            po = pbank()
                            for j4 in range(4):
                                pi = pbank()
                                for jj in range(4):
                                    j = j4 * 4 + jj
                                    mm(pi, pi.t[:, jj * 128:(jj + 1) * 128],
                                       [(ident[0:NS, j:j + 1].to_broadcast([NS, 128]), i_s[:])], [B_ident, B_is],
                                       signal=(jj == 3))
                                for jj in range(4):
                                    j = j4 * 4 + jj
                                    ts(Sn[:, j, :], S0[:, j, :], ff[:, TB + j:TB + j + 1], None, MUL, None,
                                       [B_S0, B_ff], [B_Sn])
                                    stt(Sn[:, j, :], pi.t[:, jj * 128:(jj + 1) * 128], kk[:, TB + j:TB + j + 1], Sn[:, j, :],
                                        MUL, ADD, [pi, B_kk, B_Sn], [B_Sn])
                            for j in range(NS):
                                mm(po, po.t[:, j:j + 1], [(Sn[:, j, :], qs[:, TB + j:TB + j + 1])], [B_Sn, B_qs],
                                   signal=(j == NS - 1))
                            copy(ACT, oT[:, TB:NT], po.t[:, 0:NS], [po], [B_oT])
                            dma(SP, mch[1], o_hg_s[:, hd].rearrange("t k v -> k t v"), Sn[:], reads=[B_Sn])
                        if so:
                            continue
                        act(sqt[:, 0:NC_], oT[:, 0:NC_], AF.Square, [B_oT], [B_sq])
                        for (c0, n) in blks:
                            yb = LBANK if c0 == 0 else LBANK2
                            mm(yb, yb.t[:, 0:n], [(onesb[:], sqt[:, c0:c0 + n])], [B_sq, B_onesb])
                            ts(rstd[:, c0:c0 + n], yb.t[:, 0:n], 1.0 / 128.0, EPS, MUL, ADD, [yb], [B_rstd])
                        act(rstd[:, 0:NC_], rstd[:, 0:NC_], AF.Sqrt, [B_rstd], [B_rstd])
                        run(DVE, lambda: nc.vector.reciprocal(out=rstd[:, 0:NC_], in_=rstd[:, 0:NC_]), [B_rstd], [B_rstd])
                        stt(tmpA[:, 0:NC_], oT[:, 0:NC_], misc[:, hd:hd + 1], rstd[:, 0:NC_], MUL, MUL,
                            [B_oT, B_misc, B_rstd], [B_tmpA])
                        tt(hgT[:, hd, 0:NC_], tmpA[:, 0:NC_], sog[:, 0:NC_], MUL, [B_tmpA, B_sog], [B_hg[hd]])
                    barrier()
                if DEBUG and p == 0:
                    dma(POOL, sch, dbg_s5, s5o[:], reads=B_s5o)
                if so:
                    barrier()
                    return
                with ExitStack() as gs_:
                    mg = gs_.enter_context(nc.sbuf_tensor(f"g{p}_mg", [128, 16, NT], BF16)); B_mg = [Buf() for _ in range(16)]
                    moT = gs_.enter_context(nc.sbuf_tensor(f"g{p}_mo", [128, 16, NT], F32)); B_mo = [Buf() for _ in range(16)]
                    n_extra = len(xw_ch)
                    for i in range(n_extra):
                        wslots.append((gs_.enter_context(nc.sbuf_tensor(f"g{p}_ws{i}", [128, SLOTE], BF16)), Buf(), xw_ch[i]))
                    sr = lambda k: (s5o[:, k, :], B_s5o[k])
                    gr = lambda k: (hgT[:, k, :], B_hg[k])
                    for dt in range(16):
                        b1, v1 = wload(win, 0, 16, 5120 + dt * 128, 128)
                        b2, v2 = wload(win, 0, 16, 7168 + dt * 128, 128)
                        b3, v3 = wload(wbs, 0, 8, dt * 128, 128)
                        b4, v4 = wload(wbh, 0, 8, dt * 128, 128)
                        res = {}
                        dense_tile(p, [(b1, v1, 0)], hr, lambda o: res.__setitem__("gs", o))
                        dense_tile(p, [(b3, v3, 0)], sr, lambda o: res.__setitem__("bs", o))
                        for (a, b) in zip(res["gs"], res["bs"]):
                            c0, n, pg, apg = a
                            _, _, pbb, apb = b
                            act(tmpA[:, c0:c0 + n], apg, AF.Sigmoid, [pg], [B_tmpA])
                            tt(tmpB[:, c0:c0 + n], tmpA[:, c0:c0 + n], apb, MUL, [B_tmpA, pbb], [B_tmpB])
                        dense_tile(p, [(b2, v2, 0)], hr, lambda o: res.__setitem__("gh", o))
                        dense_tile(p, [(b4, v4, 0)], gr, lambda o: res.__setitem__("bh", o))
                        for (a, b) in zip(res["gh"], res["bh"]):
                            c0, n, pg, apg = a
                            _, _, pbb, apb = b
                            act(tmpA[:, c0:c0 + n], apg, AF.Sigmoid, [pg], [B_tmpA])
                            tt(tmpA[:, c0:c0 + n], tmpA[:, c0:c0 + n], apb, MUL, [B_tmpA, pbb], [B_tmpA])
                            tt(mg[:, dt, c0:c0 + n], tmpA[:, c0:c0 + n], tmpB[:, c0:c0 + n], ADD, [B_tmpA, B_tmpB], [B_mg[dt]])
                    mr = lambda k: (mg[:, k, :], B_mg[k])
                    for dt in range(16):
                        b1, v1 = wload(wout, 0, 16, dt * 128, 128)

                        def cons_m(outs, dt=dt):
                            for (c0, n, pb, ap) in outs:
                                copy(ACT, moT[:, dt, c0:c0 + n], ap, [pb], [B_mo[dt]])
                        dense_tile(p, [(b1, v1, 0)], mr, cons_m)
                    rms_rstd(p, moT, B_mo, 16)
                    g3 = gain(3)
                    for dt in range(16):
                        stt(tmpA[:, 0:NC_], moT[:, dt, 0:NC_], g3[:, dt:dt + 1], rstd[:, 0:NC_], MUL, MUL,
                            [B_mo[dt], B_rstd, B_gains], [B_tmpA])
                        tt(xT[:, dt, 0:NC_], xT[:, dt, 0:NC_], tmpA[:, 0:NC_], ADD, [B_tmpA, B_x[dt]], [B_x[dt]])
                    barrier()
                    del wslots[-n_extra:]
                barrier()

        prep_params()
        for a in range(na):
            pa_ = 10 + a
            load_x(pa_, xa, a * TB)
            ffn(pa_, f1g, f1u, f1d, 0, 0)
            mixer(pa_, False, so=True)
        for p in range(npass):
            load_x(p, xp, p * TB)
            if stage >= 1:
                ffn(p, f1g, f1u, f1d, 0, 0)
            if stage >= 2:
                mixer(p, p == npass - 1)
            if stage >= 3:
                ffn(p, f2g, f2u, f2d, 4, 16)
            store_y(p, p * TB)

        print("COUNTS", "PE", PE.cnt, "ACT", ACT.cnt, "DVE", DVE.cnt, "POOL", POOL.cnt, "dma", [c.cnt // 16 for c in chans])
        for c in chans:
            if c.cnt:
                nc.sync.wait_ge(c.sem, c.cnt)
        for E in (PE, ACT, DVE, POOL):
            if E.cnt:
                nc.sync.wait_ge(E.sem, E.cnt)
    return nc


_NC_CACHE = {}


def _fm(v, nt):
    return np.ascontiguousarray(np.asarray(v, np.float32).reshape(nt, 128).T)


def kernel(**inp):
    f32 = np.float32
    g = lambda k: np.asarray(inp[k], f32)
    x_prompt = g("x_prompt"); x_sample = g("x_sample")
    gains = np.concatenate([_fm(g(k)[0], 16) for k in
                            ("ffn1_pre_norm", "ffn1_post_norm", "mix_pre_norm", "mix_post_norm",
                             "ffn2_pre_norm", "ffn2_post_norm")], axis=1)
    lbl = g("hgrn_lb_logits")
    misc = np.concatenate([_fm(g("hgrn_out_norm")[0], 8), _fm(lbl[0], 8), _fm(lbl[1], 8),
                           _fm(g("s5_d")[0], 8), _fm(g("s5_b_glu")[0], 8)], axis=1)
    lam = np.concatenate([_fm(g("s5_lambda_re")[0].reshape(-1), 32), _fm(g("s5_lambda_im")[0].reshape(-1), 32),
                          _fm(np.repeat(g("s5_log_dt")[0], 64), 32)], axis=1)
    bre = g("s5_b_re")[0]; bim = g("s5_b_im")[0]
    cre = g("s5_c_re")[0]; cim = g("s5_c_im")[0]
    bblk = np.zeros((8, 128, 1024), f32)
    cblk = np.zeros((8, 4, 128, 256), f32)
    for gt in range(8):
        for gl in range(8):
            G_ = gt * 8 + gl
            bblk[gt, gl * 16:(gl + 1) * 16, gl * 64:(gl + 1) * 64] = bre[G_].T
            bblk[gt, gl * 16:(gl + 1) * 16, 512 + gl * 64:512 + (gl + 1) * 64] = bim[G_].T
            j, g2 = gl // 2, gl % 2
            cblk[gt, j, g2 * 64:(g2 + 1) * 64, gl * 16:(gl + 1) * 16] = cre[G_].T
            cblk[gt, j, g2 * 64:(g2 + 1) * 64, 128 + gl * 16:128 + (gl + 1) * 16] = cim[G_].T
    ident = np.eye(128, dtype=f32)
    mask = np.triu(np.ones((128, 128), f32))
    scanm = np.ones((128, TB), f32); scanm[:, ::128] = 0.0
    oneh = np.zeros((NS, NS * 128), f32)
    for j in range(NS):
        oneh[j, j * 128:(j + 1) * 128] = 1.0
    common = {
        "f1g": g("ffn1_w_gate")[0], "f1u": g("ffn1_w_up")[0], "f1d": g("ffn1_w_down")[0],
        "f2g": g("ffn2_w_gate")[0], "f2u": g("ffn2_w_up")[0], "f2d": g("ffn2_w_down")[0],
        "win": g("w_in")[0], "wglu": g("s5_w_glu")[0], "wbs": g("w_branch_s5")[0],
        "wbh": g("w_branch_hgrn")[0], "wout": g("w_out")[0],
        "gains": gains, "misc": misc, "lam": lam, "bblk": bblk, "cblk": cblk,
        "ident": ident, "mask": mask, "scanm": scanm, "oneh": oneh,
    }
    sre = g("state_s5_re")[0].reshape(128, 4096); sim = g("state_s5_im")[0].reshape(128, 4096)
    shg = g("state_hgrn")[0]
    zero_a = np.zeros((NA * TB, D), f32)
    in_maps = []
    for c in range(8):
        m = dict(common)
        sq_, half = c // 2, c % 2
        m["xa"] = np.ascontiguousarray(x_prompt[sq_, 0:NA * TB]) if half == 1 else zero_a
        m["xp"] = np.ascontiguousarray(x_prompt[sq_, half * NB * TB:(half + 1) * NB * TB])
        m["xs"] = np.ascontiguousarray(x_sample[c * NS:(c + 1) * NS, 0])
        m["sre"] = np.ascontiguousarray(sre[c * NS:(c + 1) * NS])
        m["sim"] = np.ascontiguousarray(sim[c * NS:(c + 1) * NS])
        m["shg"] = np.ascontiguousarray(shg[c * NS:(c + 1) * NS])
        in_maps.append(m)
    if "nc" not in _NC_CACHE:
        _NC_CACHE["nc"] = build_program()
    res = run_bass_kernel_spmd(_NC_CACHE["nc"], in_maps, core_ids=list(range(8)))
    R = res.results
    _NC_CACHE["last"] = R
    y_p = np.stack([np.concatenate([R[2 * q]["yp"], R[2 * q + 1]["yp"]]) for q in range(4)]).astype(f32)
    y_s = np.concatenate([R[c]["ys"] for c in range(8)])[:, None, :].astype(f32)
    re_p = np.stack([R[2 * q + 1]["o_re_p"].reshape(64, 64) for q in range(4)])[None].astype(f32)
    im_p = np.stack([R[2 * q + 1]["o_im_p"].reshape(64, 64) for q in range(4)])[None].astype(f32)
    hg_p = np.stack([R[2 * q + 1]["o_hg_p"] for q in range(4)])[None].astype(f32)
    re_s = np.concatenate([R[c]["o_re_s"] for c in range(8)]).reshape(1, 128, 64, 64).astype(f32)
    im_s = np.concatenate([R[c]["o_im_s"] for c in range(8)]).reshape(1, 128, 64, 64).astype(f32)
    hg_s = np.concatenate([R[c]["o_hg_s"] for c in range(8)])[None].astype(f32)
    return (y_p, y_s, re_p, im_p, hg_p, re_s, im_s, hg_s)
```

```python
import math
from contextlib import ExitStack

import numpy as np
import concourse.bass as bass
import concourse.mybir as mybir
from concourse.bass_utils import run_bass_kernel_spmd

F32 = mybir.dt.float32
BF16 = mybir.dt.bfloat16
I32 = mybir.dt.int32
AF = mybir.ActivationFunctionType
ALU = mybir.AluOpType

D = 2048
DFF = 5632
NPASS = 4
NA = 2
NB = 2
TB = 512
NS = 16
NT = TB + NS
EPS = 1e-6
STAGE = 99
DEBUG = False
POOL_POST = False
SKIP = set()


class Tok:
    __slots__ = ("sem", "val")

    def __init__(self, sem, val):
        self.sem = sem
        self.val = val


class Buf:
    __slots__ = ("w", "rs", "t", "excl")

    def __init__(self, t=None, excl=False):
        self.w = None
        self.rs = {}
        self.t = t
        self.excl = excl


class Eng:
    def __init__(self, e, sem):
        self.e = e
        self.sem = sem
        self.cnt = 0
        self.seen = {}

    def wait(self, tok):
        if tok is None:
            return
        k = id(tok.sem)
        if self.seen.get(k, 0) >= tok.val:
            return
        self.e.wait_ge(tok.sem, tok.val)
        self.seen[k] = tok.val

    def sig(self, ins):
        ins.then_inc(self.sem, 1)
        self.cnt += 1
        return Tok(self.sem, self.cnt)


class Chan:
    def __init__(self, sem):
        self.sem = sem
        self.cnt = 0


def _deps(E, reads, writes):
    for b in reads:
        E.wait(b.w)
        if b.excl:
            for r in b.rs.values():
                E.wait(r)
    for b in writes:
        E.wait(b.w)
        for r in b.rs.values():
            E.wait(r)


def _reg(tok, reads, writes):
    for b in reads:
        if b.excl:
            b.w = tok
            b.rs = {}
        else:
            b.rs[id(tok.sem)] = tok
    for b in writes:
        b.w = tok
        b.rs = {}


def run(E, fn, reads=(), writes=()):
    _deps(E, reads, writes)
    tok = E.sig(fn())
    _reg(tok, reads, writes)
    return tok


def build_program(npass=NB, stage=STAGE, na=NA):
    nc = bass.Bass("TRN2", target_bir_lowering=False)
    es = ExitStack()

    def din(name, shape):
        return nc.dram_tensor(name, list(shape), F32, kind="ExternalInput").ap()

    def dout(name, shape):
        return nc.dram_tensor(name, list(shape), F32, kind="ExternalOutput").ap()

    xa = din("xa", [NA * TB, D])
    xp = din("xp", [NB * TB, D])
    xs = din("xs", [NS, D])
    sre = din("sre", [NS, 4096])
    sim = din("sim", [NS, 4096])
    shg = din("shg", [NS, 8, 128, 128])
    f1g = din("f1g", [D, DFF]); f1u = din("f1u", [D, DFF]); f1d = din("f1d", [DFF, D])
    f2g = din("f2g", [D, DFF]); f2u = din("f2u", [D, DFF]); f2d = din("f2d", [DFF, D])
    win = din("win", [D, 9216])
    wglu = din("wglu", [1024, 1024])
    wbs = din("wbs", [1024, D]); wbh = din("wbh", [1024, D]); wout = din("wout", [D, D])
    gains_d = din("gains", [128, 96])
    misc_d = din("misc", [128, 40])
    lam_d = din("lam", [128, 96])
    bblk_d = din("bblk", [8, 128, 1024])
    cblk_d = din("cblk", [8, 4, 128, 256])
    ident_d = din("ident", [128, 128])
    mask_d = din("mask", [128, 128])
    scanm_d = din("scanm", [128, TB])
    oneh_d = din("oneh", [NS, NS * 128])

    dbg_s5 = dout("dbg_s5", [128, 8, NT]) if DEBUG else None
    dbg_hg = dout("dbg_hg", [128, 8, NT]) if DEBUG else None
    yp = dout("yp", [NB * TB, D])
    ys = dout("ys", [NS, D])
    o_re_p = dout("o_re_p", [32, 128]); o_im_p = dout("o_im_p", [32, 128])
    o_hg_p = dout("o_hg_p", [8, 128, 128])
    o_re_s = dout("o_re_s", [NS, 4096]); o_im_s = dout("o_im_s", [NS, 4096])
    o_hg_s = dout("o_hg_s", [NS, 8, 128, 128])

    with es:
        def sem(name):
            return es.enter_context(nc.semaphore(name))

        def sb(name, shape, dt=F32):
            return es.enter_context(nc.sbuf_tensor("sb_" + name, list(shape), dt))

        PE = Eng(nc.tensor, sem("s_pe"))
        ACT = Eng(nc.scalar, sem("s_act"))
        DVE = Eng(nc.vector, sem("s_dve"))
        POOL = Eng(nc.gpsimd, sem("s_pool"))
        SP = Eng(nc.sync, sem("s_sp"))
        chans = []

        def chan(name):
            c = Chan(sem(name))
            chans.append(c)
            return c

        def dma(Q, ch, out, in_, reads=(), writes=(), **kw):
            _deps(Q, reads, writes)
            ins = Q.e.dma_start(out=out, in_=in_, **kw)
            ins.then_inc(ch.sem, 16)
            ch.cnt += 16
            tok = Tok(ch.sem, ch.cnt)
            _reg(tok, reads, writes)
            return tok

        banks = []
        for i in range(8):
            t = es.enter_context(nc.psum_tensor(f"ps{i}", [128, 512], F32))
            banks.append(Buf(t, excl=True))
        rr = [0]

        def pbank():
            b = banks[rr[0] % 4]
            rr[0] += 1
            return b
        srr = [0]

        def sslot():
            b = banks[4 + srr[0] % 2]
            srr[0] += 1
            return b, 0
        LBANK = banks[6]
        LBANK2 = banks[7]

        def mm(wbuf, out_ap, steps, reads, start=True, stop=True, signal=True):
            wbufs = wbuf if isinstance(wbuf, list) else [wbuf]
            _deps(PE, reads, wbufs)
            n = len(steps)
            ins = None
            for i, (l, r) in enumerate(steps):
                ins = nc.tensor.matmul(out_ap, lhsT=l, rhs=r, start=(start and i == 0),
                                       stop=(stop and i == n - 1))
            if signal:
                tok = PE.sig(ins)
                _reg(tok, reads, wbufs)
                return tok
            return None

        def tr(wbuf, out_ap, in_ap, ident_ap, reads, signal=True):
            _deps(PE, reads, [wbuf])
            ins = nc.tensor.transpose(out=out_ap, in_=in_ap, identity=ident_ap)
            if signal:
                tok = PE.sig(ins)
                _reg(tok, reads, [wbuf])
                return tok
            return None

        def barrier():
            engs = (PE, ACT, DVE, POOL, SP)
            for E in engs:
                for E2 in engs:
                    if E2 is not E and E2.cnt:
                        E.wait(Tok(E2.sem, E2.cnt))
                for c in chans:
                    if c.cnt:
                        E.wait(Tok(c.sem, c.cnt))

        def copy(E, out, in_, reads, writes):
            if E is ACT:
                return run(ACT, lambda: nc.scalar.copy(out=out, in_=in_), reads, writes)
            return run(E, lambda: E.e.tensor_copy(out=out, in_=in_), reads, writes)

        cch = chan("c_const")
        cch2 = chan("c_const2")
        gains = sb("gains", [128, 96]); B_gains = Buf()
        misc = sb("misc", [128, 40]); B_misc = Buf()
        lam = sb("lam", [128, 96]); B_lam = Buf()
        ident = sb("ident", [128, 128]); B_ident = Buf()
        identb = sb("identb", [128, 128], BF16); B_identb = Buf()
        onesb = sb("onesb", [128, 128], BF16); B_onesb = Buf()
        mask = sb("mask", [128, 128]); B_mask = Buf()
        scanm = sb("scanm", [128, TB], BF16); B_scanm = Buf()
        for (t, d, b) in ((gains, gains_d, B_gains), (misc, misc_d, B_misc), (lam, lam_d, B_lam),
                          (ident, ident_d, B_ident), (mask, mask_d, B_mask), (scanm, scanm_d, B_scanm)):
            if t is scanm:
                dma(POOL, cch2, t[:], d, writes=[b])
            else:
                dma(SP, cch, t[:], d, writes=[b])
        for b in (B_gains, B_misc, B_lam, B_ident, B_mask):
            b.w = Tok(cch.sem, cch.cnt)
        run(DVE, lambda: nc.vector.tensor_copy(out=identb[:], in_=ident[:]), [B_ident], [B_identb])
        run(DVE, lambda: nc.vector.memset(onesb[:], 1.0), [], [B_onesb])

        def gain(i):
            return gains[:, i * 16:(i + 1) * 16]

        ghalf = sb("ghalf", [128, 32]); B_ghalf = Buf()
        run(DVE, lambda: nc.vector.tensor_scalar(out=ghalf[:, 0:16], in0=gain(1), scalar1=0.5, scalar2=None,
                                                 op0=ALU.mult), [B_gains], [B_ghalf])
        run(DVE, lambda: nc.vector.tensor_scalar(out=ghalf[:, 16:32], in0=gain(5), scalar1=0.5, scalar2=None,
                                                 op0=ALU.mult), [B_gains], [B_ghalf])

        NSLOT = 4
        SLOTE = 4096
        wslots = []
        for i in range(NSLOT):
            wslots.append((sb(f"wslot{i}", [128, SLOTE], BF16), Buf(), chan(f"c_w{i}")))
        wrr = [0]
        xw_ch = [chan(f"c_wx{i}") for i in range(3)]

        def wload(W, r0, nkt, c0, ncols):
            t, b, ch = wslots[wrr[0] % len(wslots)]
            wrr[0] += 1
            assert nkt * ncols <= SLOTE
            view = t[:, 0:nkt * ncols].rearrange("p (k m) -> p k m", m=ncols)
            k0 = 0
            while k0 < nkt:
                kk = min(16, nkt - k0)
                src = W[r0 + k0 * 128:r0 + (k0 + kk) * 128, c0:c0 + ncols].rearrange("(k p) m -> p k m", p=128)
                dma(POOL, ch, view[:, k0:k0 + kk, :], src, writes=[b])
                k0 += kk
            return b, view

        xT = sb("xT", [128, 16, NT]); B_x = [Buf() for _ in range(16)]
        hT = sb("hT", [128, 16, NT], BF16); B_h = [Buf() for _ in range(16)]
        rstd = sb("rstd", [128, NT]); B_rstd = Buf()
        sqt = sb("sqt", [128, NT], BF16); B_sq = Buf()
        sqt2 = sb("sqt2", [128, NT], BF16); B_sq2 = Buf()
        tmpA = sb("tmpA", [128, NT]); B_tmpA = Buf()
        tmpB = sb("tmpB", [128, NT]); B_tmpB = Buf()

        def blocks(p):
            return [(0, TB)] + ([(TB, NS)] if p == 0 else [])

        def ncols(p):
            return NT if p == 0 else TB

        def rms_rstd(p, src, Bsrc, ntile):
            NC_ = ncols(p)
            ps = LBANK
            for kt in range(ntile):
                sq_, Bq_ = (sqt, B_sq) if kt % 2 == 0 else (sqt2, B_sq2)
                run(ACT, lambda: nc.scalar.activation(out=sq_[:, 0:NC_], in_=src[:, kt, 0:NC_], func=AF.Square),
                    [Bsrc[kt]], [Bq_])
                if p == 0:
                    mm(ps, ps.t[:, 0:TB], [(onesb[:], sq_[:, 0:TB])], [Bq_, B_onesb], start=(kt == 0),
                       stop=(kt == ntile - 1), signal=False)
                    mm([LBANK2, ps], LBANK2.t[:, 0:NS], [(onesb[:], sq_[:, TB:NT])], [Bq_, B_onesb],
                       start=(kt == 0), stop=(kt == ntile - 1))
                else:
                    mm(ps, ps.t[:, 0:TB], [(onesb[:], sq_[:, 0:TB])], [Bq_, B_onesb], start=(kt == 0),
                       stop=(kt == ntile - 1))
            dn = float(ntile * 128)
            for (c0, n) in blocks(p):
                src_ps = ps if c0 == 0 else LBANK2
                sl = src_ps.t[:, 0:n]
                run(DVE, lambda: nc.vector.tensor_scalar(out=rstd[:, c0:c0 + n], in0=sl, scalar1=1.0 / dn,
                                                         scalar2=EPS, op0=ALU.mult, op1=ALU.add),
                    [src_ps], [B_rstd])
            run(ACT, lambda: nc.scalar.activation(out=rstd[:, 0:NC_], in_=rstd[:, 0:NC_], func=AF.Sqrt),
                [B_rstd], [B_rstd])
            run(DVE, lambda: nc.vector.reciprocal(out=rstd[:, 0:NC_], in_=rstd[:, 0:NC_]), [B_rstd], [B_rstd])

        def prenorm(p, gi):
            NC_ = ncols(p)
            rms_rstd(p, xT, B_x, 16)
            g = gain(gi)
            for kt in range(16):
                run(DVE, lambda: nc.vector.scalar_tensor_tensor(
                    out=hT[:, kt, 0:NC_], in0=xT[:, kt, 0:NC_], scalar=g[:, kt:kt + 1], in1=rstd[:, 0:NC_],
                    op0=ALU.mult, op1=ALU.mult), [B_x[kt], B_rstd, B_gains], [B_h[kt]])

        def dense_tile(p, chunks, rhs, consume):
            outs = []
            for (c0, n) in blocks(p):
                if c0 == 0:
                    pb = pbank(); oap = pb.t[:, 0:n]
                else:
                    pb, s0 = sslot(); oap = pb.t[:, s0:s0 + n]
                nch = len(chunks)
                reads = []
                for ci, (wb, wv, k0) in enumerate(chunks):
                    nk = wv.shape[1]
                    steps = []
                    reads.append(wb)
                    for k in range(nk):
                        ra, rb = rhs(k0 + k)
                        steps.append((wv[:, k, :], ra[:, c0:c0 + n]))
                        reads.append(rb)
                    mm(pb, oap, steps, reads, start=(ci == 0), stop=(ci == nch - 1), signal=(ci == nch - 1))
                outs.append((c0, n, pb, oap))
            consume(outs)

        def ffn(p, wg, wu, wd, g_pre, ghalf_off):
            NC_ = ncols(p)
            with ExitStack() as fs:
                aT = fs.enter_context(nc.sbuf_tensor(f"aT{p}{ghalf_off}", [128, 44, NT], BF16))
                B_a = [Buf() for _ in range(44)]
                foT = fs.enter_context(nc.sbuf_tensor(f"foT{p}{ghalf_off}", [128, 16, NT], F32))
                B_fo = [Buf() for _ in range(16)]
                prenorm(p, g_pre)
                hr = lambda k: (hT[:, k, :], B_h[k])
                for c in range(22):
                    bg, vg = wload(wg, 0, 16, c * 256, 256)
                    bu, vu = wload(wu, 0, 16, c * 256, 256)
                    for j in range(2):
                        ft = c * 2 + j
                        res = {}
                        dense_tile(p, [(bg, vg[:, :, j * 128:(j + 1) * 128], 0)], hr, lambda o: res.__setitem__("g", o))
                        dense_tile(p, [(bu, vu[:, :, j * 128:(j + 1) * 128], 0)], hr, lambda o: res.__setitem__("u", o))
                        for (og, ou) in zip(res["g"], res["u"]):
                            c0, n, pbg, apg = og
                            _, _, pbu, apu = ou
                            run(ACT, lambda: nc.scalar.activation(out=tmpA[:, c0:c0 + n], in_=apg, func=AF.Silu),
                                [pbg], [B_tmpA])
                            run(DVE, lambda: nc.vector.tensor_tensor(out=aT[:, ft, c0:c0 + n], in0=tmpA[:, c0:c0 + n],
                                                                     in1=apu, op=ALU.mult), [B_tmpA, pbu], [B_a[ft]])
                ar = lambda k: (aT[:, k, :], B_a[k])
                for dt in range(16):
                    b0, v0 = wload(wd, 0, 22, dt * 128, 128)
                    b1, v1 = wload(wd, 22 * 128, 22, dt * 128, 128)

                    def cons(outs, dt=dt):
                        for (c0, n, pb, ap) in outs:
                            run(ACT, lambda: nc.scalar.copy(out=foT[:, dt, c0:c0 + n], in_=ap), [pb], [B_fo[dt]])
                    dense_tile(p, [(b0, v0, 0), (b1, v1, 22)], ar, cons)
                rms_rstd(p, foT, B_fo, 16)
                for dt in range(16):
                    run(DVE, lambda: nc.vector.scalar_tensor_tensor(
                        out=tmpA[:, 0:NC_], in0=foT[:, dt, 0:NC_], scalar=ghalf[:, ghalf_off + dt:ghalf_off + dt + 1],
                        in1=rstd[:, 0:NC_], op0=ALU.mult, op1=ALU.mult), [B_fo[dt], B_rstd, B_ghalf], [B_tmpA])
                    run(DVE, lambda: nc.vector.tensor_tensor(out=xT[:, dt, 0:NC_], in0=xT[:, dt, 0:NC_],
                                                             in1=tmpA[:, 0:NC_], op=ALU.add), [B_tmpA, B_x[dt]], [B_x[dt]])
                barrier()

        xch = [chan(f"c_x{i}") for i in range(2)]
        ych = [chan(f"c_y{i}") for i in range(2)]
        sch = chan("c_st")

        def load_x(p, xsrc, row0):
            with ExitStack() as ls:
                xtm = [ls.enter_context(nc.sbuf_tensor(f"xtm{p}_{i}", [128, D], F32)) for i in range(2)]
                B_xtm = [Buf(), Buf()]
                ntt = 4 + (1 if p == 0 else 0)
                for tt in range(ntt):
                    i = tt % 2
                    if tt < 4:
                        rows = 128
                        dma(SP, xch[i], xtm[i][:], xsrc[row0 + tt * 128:row0 + (tt + 1) * 128, :], writes=[B_xtm[i]])
                    else:
                        rows = NS
                        dma(SP, xch[i], xtm[i][0:NS, :], xs, writes=[B_xtm[i]])
                    for g4 in range(4):
                        pb = pbank()
                        for j in range(4):
                            dt = g4 * 4 + j
                            tr(pb, pb.t[:, j * 128:j * 128 + rows], xtm[i][0:rows, dt * 128:(dt + 1) * 128],
                               ident[0:rows, 0:rows], [B_xtm[i], B_ident], signal=(j == 3))
                        c0 = tt * 128
                        src = pb.t[:, 0:512].rearrange("p (j r) -> p j r", j=4)[:, :, 0:rows]
                        copy(ACT if g4 % 2 else DVE, xT[:, g4 * 4:(g4 + 1) * 4, c0:c0 + rows], src,
                             [pb], [B_x[g4 * 4 + j] for j in range(4)])
                barrier()

        def store_y(p, row0):
            with ExitStack() as ls:
                ytm = [ls.enter_context(nc.sbuf_tensor(f"ytm{p}_{i}", [128, D], F32)) for i in range(2)]
                B_ytm = [Buf(), Buf()]
                ntt = 4 + (1 if p == 0 else 0)
                for tt in range(ntt):
                    i = tt % 2
                    rows = 128 if tt < 4 else NS
                    c0 = tt * 128
                    for g4 in range(4):
                        pb = pbank()
                        for j in range(4):
                            dt = g4 * 4 + j
                            tr(pb, pb.t[0:rows, j * 128:(j + 1) * 128], xT[:, dt, c0:c0 + rows], ident[:],
                               [B_x[dt], B_ident], signal=(j == 3))
                        copy(ACT if g4 % 2 else DVE, ytm[i][0:rows, g4 * 512:(g4 + 1) * 512], pb.t[0:rows, :],
                             [pb], [B_ytm[i]])
                    if tt < 4:
                        dma(SP, ych[i], yp[row0 + tt * 128:row0 + (tt + 1) * 128, :], ytm[i][:], reads=[B_ytm[i]])
                    else:
                        dma(SP, ych[i], ys, ytm[i][0:NS, :], reads=[B_ytm[i]])
                barrier()

        def tt(out, a, b, op, reads, writes):
            return run(DVE, lambda: nc.vector.tensor_tensor(out=out, in0=a, in1=b, op=op), reads, writes)

        def ts(out, a, s1, s2, op0, op1, reads, writes):
            if s2 is None:
                return run(DVE, lambda: nc.vector.tensor_scalar(out=out, in0=a, scalar1=s1, scalar2=None, op0=op0),
                           reads, writes)
            return run(DVE, lambda: nc.vector.tensor_scalar(out=out, in0=a, scalar1=s1, scalar2=s2, op0=op0, op1=op1),
                       reads, writes)

        def stt(out, a, scalar, b, op0, op1, reads, writes):
            return run(DVE, lambda: nc.vector.scalar_tensor_tensor(out=out, in0=a, scalar=scalar, in1=b, op0=op0,
                                                                   op1=op1), reads, writes)

        def act(out, in_, func, reads, writes, bias=0.0, scale=1.0):
            return run(ACT, lambda: nc.scalar.activation(out=out, in_=in_, func=func, bias=bias, scale=scale),
                       reads, writes)

        MUL, ADD, SUB = ALU.mult, ALU.add, ALU.subtract

        def ttE(E, out, a, b, op, reads, writes):
            return run(E, lambda: E.e.tensor_tensor(out=out, in0=a, in1=b, op=op), reads, writes)

        def tsE(E, out, a, s1, s2, op0, op1, reads, writes):
            if s2 is None:
                return run(E, lambda: E.e.tensor_scalar(out=out, in0=a, scalar1=s1, scalar2=None, op0=op0), reads, writes)
            return run(E, lambda: E.e.tensor_scalar(out=out, in0=a, scalar1=s1, scalar2=s2, op0=op0, op1=op1), reads, writes)

        sp = sb("s5par", [128, 24, 32]); B_sp = Buf()
        (I_DT, I_AR, I_TH, I_MOD, I_CT, I_ST, I_ARE, I_AIM, I_NAIM, I_CR, I_CI, I_ICR, I_ICI, I_NICI,
         I_E128R, I_E128I, I_NE128I, I_T0, I_T1, I_T2, I_T3, I_NCI, I_T4, I_T5) = range(24)

        def P(i):
            return sp[:, i, :]
        spi = sb("s5pari", [128, 32], I32)
        wkr = sb("wkr", [128, 8, 32]); wki = sb("wki", [128, 8, 32])
        akr = sb("akr", [128, 8, 32]); aki = sb("aki", [128, 8, 32]); naki = sb("naki", [128, 8, 32])
        apr = sb("apr", [128, 4, 32]); api = sb("api", [128, 4, 32])
        e127 = sb("e127", [128, 3, 32])
        HPr = sb("HPr", [128, 32, 4]); HPi = sb("HPi", [128, 32, 4]); B_HP = Buf()
        rl_re = sb("rl_re", [128, 32]); rl_im = sb("rl_im", [128, 32]); B_rl = Buf()
        hgp = sb("hgp", [128, 16]); B_hgp = Buf()
        Shg = sb("Shg", [128, 8, 128]); B_S = [Buf() for _ in range(8)]
        R_, W_ = [B_sp, B_lam], [B_sp]

        def cmul(o_re, o_im, a_re, a_im, b_re, b_im, t0, t1):
            tt(t0, a_re, b_re, MUL, R_, W_); tt(t1, a_im, b_im, MUL, R_, W_)
            tt(o_re, t0, t1, SUB, R_, W_)
            tt(t0, a_re, b_im, MUL, R_, W_); tt(t1, a_im, b_re, MUL, R_, W_)
            tt(o_im, t0, t1, ADD, R_, W_)

        def prep_params():
            lre, lim, ldt = lam[:, 0:32], lam[:, 32:64], lam[:, 64:96]
            act(P(I_DT), ldt, AF.Exp, R_, W_)
            tt(P(I_AR), lre, P(I_DT), MUL, R_, W_)
            tt(P(I_TH), lim, P(I_DT), MUL, R_, W_)
            act(P(I_MOD), P(I_AR), AF.Exp, R_, W_)
            ts(P(I_T0), P(I_TH), 1.0 / (2 * math.pi), None, MUL, None, R_, W_)
            run(DVE, lambda: nc.vector.tensor_copy(out=spi[:], in_=P(I_T0)), R_, W_)
            run(DVE, lambda: nc.vector.tensor_copy(out=P(I_T0), in_=spi[:]), R_, W_)
            stt(P(I_T1), P(I_T0), -2 * math.pi, P(I_TH), MUL, ADD, R_, W_)
            act(P(I_ST), P(I_T1), AF.Sin, R_, W_)
            act(P(I_T2), P(I_T1), AF.Sin, R_, W_, scale=0.5)
            tt(P(I_T2), P(I_T2), P(I_T2), MUL, R_, W_)
            ts(P(I_CT), P(I_T2), -2.0, 1.0, MUL, ADD, R_, W_)
            tt(P(I_ARE), P(I_MOD), P(I_CT), MUL, R_, W_)
            tt(P(I_AIM), P(I_MOD), P(I_ST), MUL, R_, W_)
            ts(P(I_NAIM), P(I_AIM), -1.0, None, MUL, None, R_, W_)
            ts(P(I_T0), P(I_ARE), -1.0, None, ADD, None, R_, W_)
            ts(P(I_T1), lim, -1.0, None, MUL, None, R_, W_)
            cmul(P(I_CR), P(I_CI), P(I_T0), P(I_AIM), lre, P(I_T1), P(I_T2), P(I_T3))
            tt(P(I_T2), lre, lre, MUL, R_, W_); tt(P(I_T3), lim, lim, MUL, R_, W_)
            tt(P(I_T2), P(I_T2), P(I_T3), ADD, R_, W_)
            run(DVE, lambda: nc.vector.reciprocal(out=P(I_T2), in_=P(I_T2)), R_, W_)
            tt(P(I_CR), P(I_CR), P(I_T2), MUL, R_, W_); tt(P(I_CI), P(I_CI), P(I_T2), MUL, R_, W_)
            ts(P(I_NCI), P(I_CI), -1.0, None, MUL, None, R_, W_)
            tt(P(I_T2), P(I_CR), P(I_CR), MUL, R_, W_); tt(P(I_T3), P(I_CI), P(I_CI), MUL, R_, W_)
            tt(P(I_T2), P(I_T2), P(I_T3), ADD, R_, W_)
            run(DVE, lambda: nc.vector.reciprocal(out=P(I_T2), in_=P(I_T2)), R_, W_)
            tt(P(I_ICR), P(I_CR), P(I_T2), MUL, R_, W_)
            tt(P(I_NICI), P(I_CI), P(I_T2), MUL, R_, W_)
            ts(P(I_ICI), P(I_NICI), -1.0, None, MUL, None, R_, W_)
            copy(DVE, wkr[:, 0, :], P(I_CT), R_, W_); copy(DVE, wki[:, 0, :], P(I_ST), R_, W_)
            for k in range(1, 8):
                cmul(wkr[:, k, :], wki[:, k, :], wkr[:, k - 1, :], wki[:, k - 1, :], wkr[:, k - 1, :], wki[:, k - 1, :],
                     P(I_T2), P(I_T3))
            copy(DVE, P(I_E128R), wkr[:, 7, :], R_, W_); copy(DVE, P(I_E128I), wki[:, 7, :], R_, W_)
            ts(P(I_NE128I), P(I_E128I), -1.0, None, MUL, None, R_, W_)
            copy(DVE, akr[:, 0, :], P(I_ARE), R_, W_); copy(DVE, aki[:, 0, :], P(I_AIM), R_, W_)
            for k in range(1, 8):
                cmul(akr[:, k, :], aki[:, k, :], akr[:, k - 1, :], aki[:, k - 1, :], akr[:, k - 1, :], aki[:, k - 1, :],
                     P(I_T2), P(I_T3))
            ts(naki[:], aki[:], -1.0, None, MUL, None, R_, W_)
            copy(DVE, apr[:, 0, :], akr[:, 7, :], R_, W_); copy(DVE, api[:, 0, :], aki[:, 7, :], R_, W_)
            cmul(apr[:, 1, :], api[:, 1, :], apr[:, 0, :], api[:, 0, :], apr[:, 0, :], api[:, 0, :], P(I_T2), P(I_T3))
            cmul(apr[:, 2, :], api[:, 2, :], apr[:, 1, :], api[:, 1, :], apr[:, 0, :], api[:, 0, :], P(I_T2), P(I_T3))
            cmul(apr[:, 3, :], api[:, 3, :], apr[:, 1, :], api[:, 1, :], apr[:, 1, :], api[:, 1, :], P(I_T2), P(I_T3))
            ts(P(I_T4), wki[:, 0, :], -1.0, None, MUL, None, R_, W_)
            cmul(e127[:, 0, :], e127[:, 1, :], wkr[:, 7, :], wki[:, 7, :], wkr[:, 0, :], P(I_T4), P(I_T2), P(I_T3))
            ts(e127[:, 2, :], e127[:, 1, :], -1.0, None, MUL, None, R_, W_)
            run(DVE, lambda: nc.vector.memset(rl_re[:], 0.0), [], [B_rl])
            run(DVE, lambda: nc.vector.memset(rl_im[:], 0.0), [], [B_rl])
            tt(hgp[:, 8:16], misc[:, 8:16], misc[:, 16:24], SUB, [B_misc], [B_hgp])
            act(hgp[:, 0:8], hgp[:, 8:16], AF.Sigmoid, [B_hgp], [B_hgp])
            ts(hgp[:, 8:16], hgp[:, 0:8], -1.0, 1.0, MUL, ADD, [B_hgp], [B_hgp])
            for hd in range(8):
                run(DVE, lambda: nc.vector.memset(Shg[:, hd, :], 0.0), [], [B_S[hd]])

        mch = [chan(f"c_m{i}") for i in range(6)]

        def mixer(p, last, so=False):
            NC_ = ncols(p)
            blks = blocks(p)
            hr = lambda k: (hT[:, k, :], B_h[k])
            with ExitStack() as ms:
                def msb(name, shape, dt=F32):
                    return ms.enter_context(nc.sbuf_tensor(f"m{p}_{name}", list(shape), dt))
                s5o = msb("s5o", [128, 8, NT], BF16); B_s5o = [Buf() for _ in range(8)]
                prenorm(p, 2)
                with ExitStack() as ss:
                    def ssb(name, shape, dt=F32):
                        return ss.enter_context(nc.sbuf_tensor(f"s{p}_{name}", list(shape), dt))
                    ct = ssb("ct", [128, 32, 128]); st = ssb("st", [128, 32, 128]); B_tab = Buf()
                    scr = ssb("scr", [128, 32 * 64]); B_scr = Buf()
                    xrs = [ssb(f"xr{i}", [128, 2 * TB]) for i in range(2)]
                    xis = [ssb(f"xi{i}", [128, 2 * TB]) for i in range(2)]
                    B_xs_ = [Buf(), Buf()]
                    wa, wb = scr[:, 0:1024], scr[:, 1024:2048]
                    pla = ssb("pla", [128, TB]); plb = ssb("plb", [128, TB]); B_pl = Buf()
                    hre = ssb("hre", [128, 2, TB], BF16); him = ssb("him", [128, 2, TB], BF16); B_hc = Buf()
                    ini = ssb("ini", [128, 6, 2]); B_ini = Buf()
                    xs_re = ssb("xs_re", [128, 2, 4, NS]); xs_im = ssb("xs_im", [128, 2, 4, NS]); B_xsm = [Buf(), Buf()]
                    hsb_re = ssb("hsb_re", [128, 4, NS], BF16); hsb_im = ssb("hsb_im", [128, 4, NS], BF16); B_hsb = Buf()
                    u_f = ssb("u_f", [128, NT]); u_b = ssb("u_b", [128, NT], BF16); B_u = Buf()
                    ufs = [u_f, rstd]; B_uf = [Buf(), B_rstd]
                    bb = [ssb(f"bb{i}", [128, 1024], BF16) for i in range(2)]; B_bb = [Buf(), Buf()]
                    cf1 = ssb("cf", [128, 4, 256]); B_cf1 = Buf()
                    cb_re = ssb("cb_re", [128, 4, 128], BF16); cb_im = ssb("cb_im", [128, 4, 128], BF16)
                    cbr = [cb_re[:], sqt[:, 0:512].rearrange("p (j c) -> p j c", j=4)]
                    cbi = [cb_im[:], sqt2[:, 0:512].rearrange("p (j c) -> p j c", j=4)]
                    B_cb = [Buf(), Buf()]
                    zT = ssb("zT", [128, 8, NT], BF16); B_z = [Buf() for _ in range(8)]
                    yv, B_yv = tmpB, B_tmpB
                    TR, TW = [B_tab, B_sp, B_scr], [B_tab]
                    if so:
                        run(DVE, lambda: nc.vector.memset(ct[:, :, 127:128], 1.0), [], TW)
                        run(DVE, lambda: nc.vector.memset(st[:, :, 127:128], 0.0), [], TW)
                        copy(DVE, ct[:, :, 126:127], P(I_ARE).unsqueeze(2), TR, TW)
                        copy(DVE, st[:, :, 126:127], P(I_NAIM).unsqueeze(2), TR, TW)
                        for k in range(1, 7):
                            n = 1 << k
                            wr_b = akr[:, k, :].unsqueeze(2).to_broadcast([128, 32, n])
                            wi_b = naki[:, k, :].unsqueeze(2).to_broadcast([128, 32, n])
                            sc3 = scr[:, 0:32 * n].rearrange("p (a b) -> p a b", a=32)
                            src_c, src_s = ct[:, :, 128 - n:128], st[:, :, 128 - n:128]
                            dst_c, dst_s = ct[:, :, 128 - 2 * n:128 - n], st[:, :, 128 - 2 * n:128 - n]
                            tt(dst_c, src_c, wr_b, MUL, TR, TW)
                            tt(sc3, src_s, wi_b, MUL, TR, [B_scr])
                            tt(dst_c, dst_c, sc3, SUB, TR, TW)
                            tt(dst_s, src_s, wr_b, MUL, TR, TW)
                            tt(sc3, src_c, wi_b, MUL, TR, [B_scr])
                            tt(dst_s, dst_s, sc3, ADD, TR, TW)
                    else:
                        run(DVE, lambda: nc.vector.memset(ct[:, :, 0:1], 1.0), [], TW)
                        run(DVE, lambda: nc.vector.memset(st[:, :, 0:1], 0.0), [], TW)
                        copy(DVE, ct[:, :, 1:2], P(I_CT).unsqueeze(2), TR, TW)
                        copy(DVE, st[:, :, 1:2], P(I_ST).unsqueeze(2), TR, TW)
                        for k in range(1, 7):
                            n = 1 << k
                            wr_b = wkr[:, k, :].unsqueeze(2).to_broadcast([128, 32, n])
                            wi_b = wki[:, k, :].unsqueeze(2).to_broadcast([128, 32, n])
                            sc3 = scr[:, 0:32 * n].rearrange("p (a b) -> p a b", a=32)
                            tt(ct[:, :, n:2 * n], ct[:, :, 0:n], wr_b, MUL, TR, TW)
                            tt(sc3, st[:, :, 0:n], wi_b, MUL, TR, [B_scr])
                            tt(ct[:, :, n:2 * n], ct[:, :, n:2 * n], sc3, SUB, TR, TW)
                            tt(st[:, :, n:2 * n], st[:, :, 0:n], wr_b, MUL, TR, TW)
                            tt(sc3, ct[:, :, 0:n], wi_b, MUL, TR, [B_scr])
                            tt(st[:, :, n:2 * n], st[:, :, n:2 * n], sc3, ADD, TR, TW)
                    if p == 0:
                        h0re = ssb("h0re", [128, 32, NS]); h0im = ssb("h0im", [128, 32, NS]); B_h0 = Buf()
                        hsre, hsim, B_hs = h0re, h0im, B_h0
                        stg = [pla[0:NS, :], plb[0:NS, :]]; B_stg = [B_pl, B_pl]
                        n_ld = 0
                        for (src_d, dst) in ((sre, h0re), (sim, h0im)):
                            for c8 in range(8):
                                i = n_ld % 2; n_ld += 1
                                dma(SP, mch[i], stg[i][:], src_d[:, c8 * 512:(c8 + 1) * 512], writes=[B_stg[i]])
                                pb = pbank()
                                for j in range(4):
                                    tr(pb, pb.t[:, j * NS:(j + 1) * NS], stg[i][:, j * 128:(j + 1) * 128],
                                       ident[0:NS, 0:NS], [B_stg[i], B_ident], signal=(j == 3))
                                copy(DVE, dst[:, c8 * 4:(c8 + 1) * 4, :],
                                     pb.t[:, 0:4 * NS].rearrange("p (a b) -> p a b", a=4), [pb], [B_h0])
                        icr_b = P(I_ICR).unsqueeze(2).to_broadcast([128, 32, NS])
                        ici_b = P(I_ICI).unsqueeze(2).to_broadcast([128, 32, NS])
                        s3 = scr[:, 0:32 * NS].rearrange("p (a b) -> p a b", a=32)
                        s3b = scr[:, 1024:1024 + 32 * NS].rearrange("p (a b) -> p a b", a=32)
                        HR = [B_h0, B_sp, B_scr]
                        s3c = scr[:, 512:512 + 32 * NS].rearrange("p (a b) -> p a b", a=32)
                        tt(s3, h0re[:], icr_b, MUL, HR, [B_scr]); tt(s3b, h0im[:], ici_b, MUL, HR, [B_scr])
                        tt(s3c, s3, s3b, SUB, HR, [B_scr])
                        tt(s3, h0re[:], ici_b, MUL, HR, [B_scr]); tt(s3b, h0im[:], icr_b, MUL, HR, [B_scr])
                        tt(h0im[:], s3, s3b, ADD, HR, [B_h0])
                        copy(DVE, h0re[:], s3c, [B_scr], [B_h0])
                    def prep_gt(gt):
                        ib = gt % 2
                        g4 = slice(gt * 4, gt * 4 + 4)
                        dma(POOL, mch[2 + ib], bb[ib][:], bblk_d[gt], writes=[B_bb[ib]])
                        wb_, wv_ = wload(win, 0, 16, gt * 128, 128)
                        if not so:
                            dma(SP, mch[4], cf1[:], cblk_d[gt].rearrange("j p c -> p j c"), writes=[B_cf1])
                            cr_, ci_ = cf1[:, :, 0:128], cf1[:, :, 128:256]
                            crb = P(I_CR)[:, g4].unsqueeze(2).to_broadcast([128, 4, 128])
                            cib = P(I_CI)[:, g4].unsqueeze(2).to_broadcast([128, 4, 128])
                            wa3 = pla[:].rearrange("p (j c) -> p j c", j=4)
                            wb3 = plb[:].rearrange("p (j c) -> p j c", j=4)
                            CR = [B_cf1, B_sp, B_pl]
                            ttE(DVE, wa3, cr_, crb, MUL, CR, [B_pl]); ttE(DVE, wb3, ci_, cib, MUL, CR, [B_pl])
                            ttE(DVE, cbr[ib], wa3, wb3, SUB, CR, [B_cb[ib]])
                            ttE(DVE, wa3, cr_, cib, MUL, CR, [B_pl]); ttE(DVE, wb3, ci_, crb, MUL, CR, [B_pl])
                            ttE(DVE, wa3, wa3, wb3, ADD, CR, [B_pl])
                            tsE(DVE, cbi[ib], wa3, -1.0, None, MUL, None, CR, [B_cb[ib]])

                        def cons_u(outs):
                            for (c0, n, pb, ap) in outs:
                                copy(ACT, ufs[ib][:, c0:c0 + n], ap, [pb], [B_uf[ib]])
                                copy(ACT, u_b[:, c0:c0 + n], ap, [pb], [B_u])
                        dense_tile(p, [(wb_, wv_, 0)], hr, cons_u)

                    def produce_X(gt, half, s_):
                        ib = gt % 2
                        for jj in range(2):
                            j = half * 2 + jj
                            for (c0, n) in blks:
                                for comp in range(2):
                                    if c0 == 0:
                                        pb = pbank()
                                    else:
                                        pb, _ = sslot()
                                    oap = pb.t[:, 0:n]
                                    mm(pb, oap, [(bb[ib][:, comp * 512 + j * 128:comp * 512 + (j + 1) * 128],
                                                  u_b[:, c0:c0 + n])], [B_bb[ib], B_u])
                                    if c0 == 0:
                                        dst = (xrs[s_] if comp == 0 else xis[s_])[:, jj * TB:(jj + 1) * TB]
                                        copy(ACT, dst, oap, [pb], [B_xs_[s_]])
                                    else:
                                        dst = (xs_re if comp == 0 else xs_im)[:, ib, j, :]
                                        copy(ACT, dst, oap, [pb], [B_xsm[ib]])

                    def consume(gt, half, s_):
                        ib = gt % 2
                        tl0 = gt * 4 + half * 2
                        t2 = slice(tl0, tl0 + 2)
                        xr, xi, B_xr = xrs[s_], xis[s_], B_xs_[s_]
                        x4r = xr[:].rearrange("p (j k t) -> p j k t", j=2, k=4)
                        x4i = xi[:].rearrange("p (j k t) -> p j k t", j=2, k=4)
                        a4 = wa[:].rearrange("p (j k t) -> p j k t", j=2, k=4)
                        b4 = wb[:].rearrange("p (j k t) -> p j k t", j=2, k=4)
                        ctb = ct[:, t2, :].unsqueeze(2).to_broadcast([128, 2, 4, 128])
                        stb = st[:, t2, :].unsqueeze(2).to_broadcast([128, 2, 4, 128])
                        XR = [B_xr, B_tab, B_scr]
                        tt(a4, x4r, ctb, MUL, XR, [B_scr]); tt(b4, x4i, stb, MUL, XR, [B_scr])
                        tt(a4, a4, b4, ADD, XR, [B_scr])
                        tt(b4, x4i, ctb, MUL, XR, [B_scr]); tt(x4r, x4r, stb, MUL, XR, [B_xr])
                        tt(b4, b4, x4r, SUB, XR, [B_scr])
                        if so:
                            run(DVE, lambda: nc.vector.tensor_reduce(out=HPr[:, t2, :], in_=a4, axis=mybir.AxisListType.X,
                                                                     op=ADD), [B_scr], [B_HP])
                            run(DVE, lambda: nc.vector.tensor_reduce(out=HPi[:, t2, :], in_=b4, axis=mybir.AxisListType.X,
                                                                     op=ADD), [B_scr], [B_HP])
                            return
                        for k in range(4):
                            if k == 0:
                                lr, li, LB = rl_re[:, t2], rl_im[:, t2], B_rl
                            else:
                                lr, li, LB = x4i[:, :, k - 1, 127], x4r[:, :, k - 1, 127], B_xr
                            IR = [LB, B_sp, B_ini]
                            er, ei = P(I_E128R)[:, t2], P(I_E128I)[:, t2]
                            tt(ini[:, 0, :], lr, er, MUL, IR, [B_ini]); tt(ini[:, 1, :], li, ei, MUL, IR, [B_ini])
                            tt(ini[:, 2, :], li, er, MUL, IR, [B_ini]); tt(ini[:, 3, :], lr, ei, MUL, IR, [B_ini])
                            tt(ini[:, 4, :], ini[:, 0, :], ini[:, 1, :], SUB, IR, [B_ini])
                            tt(ini[:, 5, :], ini[:, 2, :], ini[:, 3, :], ADD, IR, [B_ini])
                            for jj in range(2):
                                mod_b = P(I_MOD)[:, tl0 + jj:tl0 + jj + 1].to_broadcast([128, 128])
                                run(DVE, lambda: nc.vector.tensor_tensor_scan(
                                    out=x4i[:, jj, k, :], data0=mod_b, data1=a4[:, jj, k, :], initial=ini[:, 4, jj:jj + 1],
                                    op0=MUL, op1=ADD), [B_scr, B_ini, B_sp], [B_xr])
                                run(DVE, lambda: nc.vector.tensor_tensor_scan(
                                    out=x4r[:, jj, k, :], data0=mod_b, data1=b4[:, jj, k, :], initial=ini[:, 5, jj:jj + 1],
                                    op0=MUL, op1=ADD), [B_scr, B_ini, B_sp], [B_xr])
                        copy(DVE, rl_re[:, t2], x4i[:, :, 3, 127], [B_xr], [B_rl])
                        copy(DVE, rl_im[:, t2], x4r[:, :, 3, 127], [B_xr], [B_rl])
                        h4r = hre[:].rearrange("p j (k t) -> p j k t", k=4)
                        h4i = him[:].rearrange("p j (k t) -> p j k t", k=4)
                        tt(a4, x4i, ctb, MUL, XR, [B_scr]); tt(b4, x4r, stb, MUL, XR, [B_scr])
                        tt(h4r, a4, b4, SUB, [B_scr], [B_hc])
                        tt(a4, x4i, stb, MUL, XR, [B_scr]); tt(b4, x4r, ctb, MUL, XR, [B_scr])
                        tt(h4i, a4, b4, ADD, [B_scr], [B_hc])
                        YB = LBANK if ib == 0 else LBANK2
                        for jj in range(2):
                            j = half * 2 + jj
                            mm(YB, YB.t[:, 0:TB], [(cbr[ib][:, j, :], hre[:, jj, :]), (cbi[ib][:, j, :], him[:, jj, :])],
                               [B_cb[ib], B_hc], start=(j == 0), stop=(j == 3))

                    def epilogue(gt):
                        ib = gt % 2
                        g4 = slice(gt * 4, gt * 4 + 4)
                        if p == 0:
                            h0r, h0i = h0re[:, g4, :], h0im[:, g4, :]
                            areb = P(I_ARE)[:, g4].unsqueeze(2).to_broadcast([128, 4, NS])
                            aimb = P(I_AIM)[:, g4].unsqueeze(2).to_broadcast([128, 4, NS])
                            sA = wa[:, 0:4 * NS].rearrange("p (j c) -> p j c", j=4)
                            sB = wa[:, 64:64 + 4 * NS].rearrange("p (j c) -> p j c", j=4)
                            sC = wa[:, 128:128 + 4 * NS].rearrange("p (j c) -> p j c", j=4)
                            sD = wa[:, 192:192 + 4 * NS].rearrange("p (j c) -> p j c", j=4)
                            SR = [B_h0, B_sp, B_scr, B_xsm[ib]]
                            tt(sA, h0r, areb, MUL, SR, [B_scr]); tt(sB, h0i, aimb, MUL, SR, [B_scr])
                            tt(sC, h0i, areb, MUL, SR, [B_scr]); tt(sD, h0r, aimb, MUL, SR, [B_scr])
                            tt(sA, sA, sB, SUB, SR, [B_scr]); tt(sC, sC, sD, ADD, SR, [B_scr])
                            tt(h0r, sA, xs_re[:, ib], ADD, SR, [B_h0]); tt(h0i, sC, xs_im[:, ib], ADD, SR, [B_h0])
                            copy(DVE, hsb_re[:], h0r, [B_h0], [B_hsb]); copy(DVE, hsb_im[:], h0i, [B_h0], [B_hsb])
                            ysb, _ = sslot()
                            for j in range(4):
                                mm(ysb, ysb.t[:, 0:NS], [(cbr[ib][:, j, :], hsb_re[:, j, :]), (cbi[ib][:, j, :], hsb_im[:, j, :])],
                                   [B_cb[ib], B_hsb], start=(j == 0), stop=(j == 3))
                        for (c0, n) in blks:
                            yb = (LBANK if ib == 0 else LBANK2) if c0 == 0 else ysb
                            stt(yv[:, c0:c0 + n], ufs[ib][:, c0:c0 + n], misc[:, 24 + gt:25 + gt], yb.t[:, 0:n], MUL, ADD,
                                [B_uf[ib], B_misc, yb], [B_yv])
                        yy = yv[:, 0:NC_]; ta = tmpA[:, 0:NC_]
                        ttE(DVE, ta, yy, yy, MUL, [B_yv], [B_tmpA])
                        tsE(DVE, ta, ta, 0.044715, 1.0, MUL, ADD, [B_tmpA], [B_tmpA])
                        ttE(DVE, ta, ta, yy, MUL, [B_yv, B_tmpA], [B_tmpA])
                        act(ta, ta, AF.Sigmoid, [B_tmpA], [B_tmpA], scale=1.5957691216057308)
                        ttE(DVE, zT[:, gt, 0:NC_], yy, ta, MUL, [B_yv, B_tmpA], [B_z[gt]])

                    halves = [(g_, h_) for g_ in range(8) for h_ in range(2)]
                    prep_gt(0)
                    produce_X(0, 0, 0)
                    for idx, (gt, half) in enumerate(halves):
                        if idx + 1 < len(halves):
                            ng, nh = halves[idx + 1]
                            if nh == 0:
                                prep_gt(ng)
                            produce_X(ng, nh, (idx + 1) % 2)
                        consume(gt, half, idx % 2)
                        if not so and half == 0 and gt > 0:
                            epilogue(gt - 1)
                    if not so:
                        epilogue(7)
                    if so:
                        SW = [B_sp, B_rl, B_HP, B_scr]
                        q = [scr[:, i * 32:(i + 1) * 32] for i in range(8)]

                        def cm(o_r, o_i, a_r, a_i, b_r, b_i):
                            tt(q[4], a_r, b_r, MUL, SW, [B_scr]); tt(q[5], a_i, b_i, MUL, SW, [B_scr])
                            tt(q[6], a_r, b_i, MUL, SW, [B_scr]); tt(q[7], a_i, b_r, MUL, SW, [B_scr])
                            tt(o_r, q[4], q[5], SUB, SW, [B_scr]); tt(o_i, q[6], q[7], ADD, SW, [B_scr])
                        cm(q[0], q[1], rl_re[:], rl_im[:], e127[:, 0, :], e127[:, 1, :])
                        cm(q[2], q[3], q[0], q[1], apr[:, 3, :], api[:, 3, :])
                        for k in range(3):
                            cm(q[0], q[1], HPr[:, :, k], HPi[:, :, k], apr[:, 2 - k, :], api[:, 2 - k, :])
                            tt(q[2], q[2], q[0], ADD, SW, [B_scr]); tt(q[3], q[3], q[1], ADD, SW, [B_scr])
                        tt(q[2], q[2], HPr[:, :, 3], ADD, SW, [B_scr]); tt(q[3], q[3], HPi[:, :, 3], ADD, SW, [B_scr])
                        tt(q[4], q[2], e127[:, 0, :], MUL, SW, [B_scr]); tt(q[5], q[3], e127[:, 2, :], MUL, SW, [B_scr])
                        tt(q[6], q[2], e127[:, 2, :], MUL, SW, [B_scr]); tt(q[7], q[3], e127[:, 0, :], MUL, SW, [B_scr])
                        tt(rl_re[:], q[4], q[5], SUB, SW, [B_rl]); tt(rl_im[:], q[6], q[7], ADD, SW, [B_rl])
                    zr = lambda k: (zT[:, k, :], B_z[k])
                    for mt in (range(8) if not so else []):
                        wb_, wv_ = wload(wglu, 0, 8, mt * 128, 128)

                        def cons_g(outs, mt=mt):
                            for (c0, n, pb, ap) in outs:
                                act(tmpA[:, c0:c0 + n], ap, AF.Sigmoid, [pb, B_misc], [B_tmpA],
                                    bias=misc[:, 32 + mt:33 + mt])
                                tt(s5o[:, mt, c0:c0 + n], zT[:, mt, c0:c0 + n], tmpA[:, c0:c0 + n], MUL,
                                   [B_z[mt], B_tmpA], [B_s5o[mt]])
                        dense_tile(p, [(wb_, wv_, 0)], zr, cons_g)
                    if DEBUG and p == 0:
                        dma(POOL, sch, dbg_hg, zT[:], reads=B_z)
                    if last:
                        E127r, E127i = ct[:, :, 127], st[:, :, 127]
                        FR = [B_rl, B_tab, B_sp, B_scr]
                        a0, a1, a2, a3 = (scr[:, 0:32], scr[:, 32:64], scr[:, 64:96], scr[:, 96:128])
                        tt(a0, rl_re[:], E127r, MUL, FR, [B_scr]); tt(a1, rl_im[:], E127i, MUL, FR, [B_scr])
                        tt(a2, a0, a1, SUB, FR, [B_scr])
                        tt(a0, rl_re[:], E127i, MUL, FR, [B_scr]); tt(a1, rl_im[:], E127r, MUL, FR, [B_scr])
                        tt(a3, a0, a1, ADD, FR, [B_scr])
                        o0, o1 = scr[:, 128:160], scr[:, 160:192]
                        tt(a0, a2, P(I_CR), MUL, FR, [B_scr]); tt(a1, a3, P(I_CI), MUL, FR, [B_scr])
                        tt(o0, a0, a1, SUB, FR, [B_scr])
                        tt(a0, a2, P(I_CI), MUL, FR, [B_scr]); tt(a1, a3, P(I_CR), MUL, FR, [B_scr])
                        tt(o1, a0, a1, ADD, FR, [B_scr])
                        dma(SP, sch, o_re_p.rearrange("t p -> p t"), o0, reads=[B_scr], allow_slow_non_contiguous=True)
                        dma(SP, sch, o_im_p.rearrange("t p -> p t"), o1, reads=[B_scr], allow_slow_non_contiguous=True)
                    if p == 0:
                        cr_b = P(I_CR).unsqueeze(2).to_broadcast([128, 32, NS])
                        ci_b = P(I_CI).unsqueeze(2).to_broadcast([128, 32, NS])
                        HR = [B_hs, B_sp, B_scr, B_h0]
                        tt(s3, hsre[:], cr_b, MUL, HR, [B_scr]); tt(s3b, hsim[:], ci_b, MUL, HR, [B_scr])
                        tt(s3c, s3, s3b, SUB, HR, [B_scr])
                        tt(s3, hsre[:], ci_b, MUL, HR, [B_scr]); tt(s3b, hsim[:], cr_b, MUL, HR, [B_scr])
                        tt(h0im[:], s3, s3b, ADD, HR, [B_h0])
                        copy(DVE, h0re[:], s3c, [B_scr], [B_h0])
                        n_st = 0
                        for (src_t, dst_d) in ((h0re, o_re_s), (h0im, o_im_s)):
                            for c8 in range(8):
                                i = n_st % 2; n_st += 1
                                pb = pbank()
                                for j in range(4):
                                    tr(pb, pb.t[0:NS, j * 128:(j + 1) * 128], src_t[:, c8 * 4 + j, :], ident[:],
                                       [B_h0, B_ident], signal=(j == 3))
                                copy(ACT, stg[i][:], pb.t[0:NS, :], [pb], [B_stg[i]])
                                dma(SP, mch[i], dst_d[:, c8 * 512:(c8 + 1) * 512], stg[i][:], reads=[B_stg[i]])
                    barrier()
                hgT = msb("hgT", [128, 8, NT], BF16); B_hg = [Buf() for _ in range(8)]
                with ExitStack() as hs:
                    def hsb(name, shape, dt=F32):
                        return hs.enter_context(nc.sbuf_tensor(f"h{p}_{name}", list(shape), dt))
                    qs = hsb("qs", [128, NT]); B_qs = Buf()
                    ff = hsb("ff", [128, NT]); B_ff = Buf()
                    lf = hsb("lf", [128, NT]); B_lf = Buf()
                    kk = hsb("kk", [128, NT]); B_kk = Buf()
                    sog = hsb("sog", [128, NT]); B_sog = Buf()
                    G = hsb("G", [128, TB]); B_G = Buf()
                    Gp = hsb("Gp", [128, TB]); B_Gp = Buf()
                    eg = hsb("eg", [128, TB]); B_eg = Buf()
                    qt = hsb("qt", [128, TB], BF16); kt_ = hsb("kt", [128, TB], BF16); B_qk = Buf()
                    ecol = hsb("ecol", [128, 12]); B_ecol = Buf()
                    itm = hsb("itm", [128, 4, 128], BF16); B_itm = Buf()
                    i_s = hsb("i_s", [NS, 128]); B_is = Buf()
                    oT = hsb("oT", [128, NT]); B_oT = Buf()
                    attT = hsb("attT", [128, 128], BF16); B_att = Buf()
                    ktm = hsb("ktm", [128, 128], BF16); B_ktm = Buf()
                    Sb = hsb("Sb", [128, 128], BF16); B_Sb = Buf()
                    run(DVE, lambda: nc.vector.memset(attT[:], 0.0), [], [B_att])
                    if not so:
                        klo = hsb("klo", [128, TB], BF16); khi = hsb("khi", [128, TB], BF16); qhi = hsb("qhi", [128, TB], BF16)
                        B_kq2 = Buf()
                        for t_ in (klo, khi, qhi):
                            run(DVE, lambda: nc.vector.memset(t_[:], 0.0), [], [B_kq2])
                    if p == 0:
                        S0 = hsb("S0", [128, NS, 128]); B_S0 = Buf()
                        Sn = hsb("Sn", [128, NS, 128]); B_Sn = Buf()
                    for hd in range(8):
                        if not so:
                            bq, vq = wload(win, 0, 16, 1024 + hd * 128, 128)
                        bf_, vf = wload(win, 0, 16, 2048 + hd * 128, 128)
                        bi, vi_ = wload(win, 0, 16, 3072 + hd * 128, 128)
                        if not so:
                            bo, vo = wload(win, 0, 16, 4096 + hd * 128, 128)
                        lbc, omc = hgp[:, hd:hd + 1], hgp[:, 8 + hd:9 + hd]

                        def cons_q(outs):
                            for (c0, n, pb, ap) in outs:
                                act(qs[:, c0:c0 + n], ap, AF.Silu, [pb], [B_qs])

                        def cons_f(outs):
                            for (c0, n, pb, ap) in outs:
                                act(ff[:, c0:c0 + n], ap, AF.Sigmoid, [pb], [B_ff])
                        def cons_o(outs):
                            for (c0, n, pb, ap) in outs:
                                act(sog[:, c0:c0 + n], ap, AF.Silu, [pb], [B_sog])
                        if not so:
                            dense_tile(p, [(bq, vq, 0)], hr, cons_q)
                        dense_tile(p, [(bf_, vf, 0)], hr, cons_f)
                        if not so:
                            dense_tile(p, [(bo, vo, 0)], hr, cons_o)
                        ts(ff[:, 0:NC_], ff[:, 0:NC_], omc, lbc, MUL, ADD, [B_ff, B_hgp], [B_ff])
                        act(lf[:, 0:NC_], ff[:, 0:NC_], AF.Ln, [B_ff], [B_lf])
                        ts(kk[:, 0:NC_], ff[:, 0:NC_], -1.0, 1.0, MUL, ADD, [B_ff], [B_kk])
                        for tk in range(4):
                            pb = pbank()
                            mm(pb, pb.t[:, 0:128], [(hT[:, k, tk * 128:(tk + 1) * 128], vi_[:, k, :]) for k in range(16)],
                               [bi] + B_h)
                            copy(ACT, itm[:, tk, :], pb.t[:, 0:128], [pb], [B_itm])
                        if p == 0:
                            pb, _ = sslot()
                            mm(pb, pb.t[0:NS, 0:128], [(hT[:, k, TB:NT], vi_[:, k, :]) for k in range(16)], [bi] + B_h)
                            copy(ACT, i_s[:], pb.t[0:NS, 0:128], [pb], [B_is])
                        run(DVE, lambda: nc.vector.tensor_tensor_scan(out=G[:], data0=scanm[:], data1=lf[:, 0:TB],
                                                                      initial=0.0, op0=MUL, op1=ADD),
                            [B_lf, B_scanm], [B_G])
                        G3 = G[:].rearrange("p (k t) -> p k t", k=4)
                        Gp3 = Gp[:].rearrange("p (k t) -> p k t", k=4)
                        tt(Gp3, G3, G3[:, :, 63:64].to_broadcast([128, 4, 128]), SUB, [B_G], [B_Gp])
                        act(ecol[:, 0:4], G3[:, :, 127], AF.Exp, [B_G], [B_ecol])
                        act(ecol[:, 8:12], Gp3[:, :, 127], AF.Exp, [B_Gp], [B_ecol])
                        if not so:
                            act(ecol[:, 4:8], G3[:, :, 63], AF.Exp, [B_G], [B_ecol])
                            act(eg[:], Gp[:], AF.Exp, [B_Gp], [B_eg])
                            tt(qt[:], qs[:, 0:TB], eg[:], MUL, [B_qs, B_eg], [B_qk])
                            q3_ = qs[:, 0:TB].rearrange("p (k t) -> p k t", k=4); e3_ = eg[:].rearrange("p (k t) -> p k t", k=4)
                            tt(qhi[:].rearrange("p (k t) -> p k t", k=4)[:, :, 64:128], q3_[:, :, 64:128], e3_[:, :, 64:128], MUL,
                               [B_qs, B_eg], [B_kq2])
                        act(eg[:], Gp[:], AF.Exp, [B_Gp, B_qk], [B_eg], scale=-1.0)
                        tt(kt_[:], kk[:, 0:TB], eg[:], MUL, [B_kk, B_eg], [B_qk])
                        if not so:
                            k3_ = kk[:, 0:TB].rearrange("p (k t) -> p k t", k=4); e3_ = eg[:].rearrange("p (k t) -> p k t", k=4)
                            tt(klo[:].rearrange("p (k t) -> p k t", k=4)[:, :, 0:64], k3_[:, :, 0:64], e3_[:, :, 0:64], MUL,
                               [B_kk, B_eg], [B_kq2])
                            tt(khi[:].rearrange("p (k t) -> p k t", k=4)[:, :, 64:128], k3_[:, :, 64:128], e3_[:, :, 64:128], MUL,
                               [B_kk, B_eg], [B_kq2])
                        for k in range(4):
                            sl = slice(k * 128, (k + 1) * 128)
                            if not so:
                                pa = pbank()
                                mm(pa, pa.t[:, 0:128], [(klo[:, sl], qt[:, sl]), (khi[:, sl], qhi[:, sl])], [B_qk, B_kq2])
                                run(DVE, lambda: nc.vector.copy_predicated(out=attT[:], mask=mask[:].bitcast(I32), data=pa.t[:, 0:128]),
                                    [pa, B_mask], [B_att])
                                ts(Sb[:], Shg[:, hd, :], ecol[:, 4 + k:5 + k], None, MUL, None, [B_S[hd], B_ecol], [B_Sb])
                                po = pbank()
                                mm(po, po.t[:, 0:128], [(itm[:, k, :], attT[:]), (Sb[:], qt[:, sl])],
                                   [B_itm, B_att, B_Sb, B_qk])
                                copy(ACT, oT[:, sl], po.t[:, 0:128], [po], [B_oT])
                            pt = pbank()
                            ptb = pt.t[:].bitcast(BF16)
                            tr(pt, ptb[:, 0:128], kt_[:, sl], identb[:], [B_qk, B_identb])
                            copy(DVE, ktm[:], ptb[:, 0:128], [pt], [B_ktm])
                            pd = pbank()
                            mm(pd, pd.t[:, 0:128], [(ktm[:], itm[:, k, :])], [B_ktm, B_itm])
                            ts(Shg[:, hd, :], Shg[:, hd, :], ecol[:, k:k + 1], None, MUL, None, [B_S[hd], B_ecol], [B_S[hd]])
                            stt(Shg[:, hd, :], pd.t[:, 0:128], ecol[:, 8 + k:9 + k], Shg[:, hd, :], MUL, ADD,
                                [pd, B_ecol, B_S[hd]], [B_S[hd]])
                        if last:
                            dma(SP, sch, o_hg_p[hd], Shg[:, hd, :], reads=[B_S[hd]])
                        if p == 0:
                            dma(SP, mch[0], S0[:], shg[:, hd].rearrange("t k v -> k t v"), writes=[B_S0])
                            po = pbank()
                            for j4 in range(4):
                                pi = pbank()
                                for jj in range(4):
                                    j = j4 * 4 + jj
                                    mm(pi, pi.t[:, jj * 128:(jj + 1) * 128],
                                       [(ident[0:NS, j:j + 1].to_broadcast([NS, 128]), i_s[:])], [B_ident, B_is],
                                       signal=(jj == 3))
                                for jj in range(4):
                                    j = j4 * 4 + jj
                                    ts(Sn[:, j, :], S0[:, j, :], ff[:, TB + j:TB + j + 1], None, MUL, None,
                                       [B_S0, B_ff], [B_Sn])
                                    stt(Sn[:, j, :], pi.t[:, jj * 128:(jj + 1) * 128], kk[:, TB + j:TB + j + 1], Sn[:, j, :],
                                        MUL, ADD, [pi, B_kk, B_Sn], [B_Sn])
                            for j in range(NS):
                                mm(po, po.t[:, j:j + 1], [(Sn[:, j, :], qs[:, TB + j:TB + j + 1])], [B_Sn, B_qs],
                                   signal=(j == NS - 1))
                            copy(ACT, oT[:, TB:NT], po.t[:, 0:NS], [po], [B_oT])
                            dma(SP, mch[1], o_hg_s[:, hd].rearrange("t k v -> k t v"), Sn[:], reads=[B_Sn])
                        if so:
                            continue
                        act(sqt[:, 0:NC_], oT[:, 0:NC_], AF.Square, [B_oT], [B_sq])
                        for (c0, n) in blks:
                            yb = LBANK if c0 == 0 else LBANK2
                            mm(yb, yb.t[:, 0:n], [(onesb[:], sqt[:, c0:c0 + n])], [B_sq, B_onesb])
                            ts(rstd[:, c0:c0 + n], yb.t[:, 0:n], 1.0 / 128.0, EPS, MUL, ADD, [yb], [B_rstd])
                        act(rstd[:, 0:NC_], rstd[:, 0:NC_], AF.Sqrt, [B_rstd], [B_rstd])
                        run(DVE, lambda: nc.vector.reciprocal(out=rstd[:, 0:NC_], in_=rstd[:, 0:NC_]), [B_rstd], [B_rstd])
                        stt(tmpA[:, 0:NC_], oT[:, 0:NC_], misc[:, hd:hd + 1], rstd[:, 0:NC_], MUL, MUL,
                            [B_oT, B_misc, B_rstd], [B_tmpA])
                        tt(hgT[:, hd, 0:NC_], tmpA[:, 0:NC_], sog[:, 0:NC_], MUL, [B_tmpA, B_sog], [B_hg[hd]])
                    barrier()
                if DEBUG and p == 0:
                    dma(POOL, sch, dbg_s5, s5o[:], reads=B_s5o)
                if so:
                    barrier()
                    return
                with ExitStack() as gs_:
                    mg = gs_.enter_context(nc.sbuf_tensor(f"g{p}_mg", [128, 16, NT], BF16)); B_mg = [Buf() for _ in range(16)]
                    moT = gs_.enter_context(nc.sbuf_tensor(f"g{p}_mo", [128, 16, NT], F32)); B_mo = [Buf() for _ in range(16)]
                    n_extra = len(xw_ch)
                    for i in range(n_extra):
                        wslots.append((gs_.enter_context(nc.sbuf_tensor(f"g{p}_ws{i}", [128, SLOTE], BF16)), Buf(), xw_ch[i]))
                    sr = lambda k: (s5o[:, k, :], B_s5o[k])
                    gr = lambda k: (hgT[:, k, :], B_hg[k])
                    for dt in range(16):
                        b1, v1 = wload(win, 0, 16, 5120 + dt * 128, 128)
                        b2, v2 = wload(win, 0, 16, 7168 + dt * 128, 128)
                        b3, v3 = wload(wbs, 0, 8, dt * 128, 128)
                        b4, v4 = wload(wbh, 0, 8, dt * 128, 128)
                        res = {}
                        dense_tile(p, [(b1, v1, 0)], hr, lambda o: res.__setitem__("gs", o))
                        dense_tile(p, [(b3, v3, 0)], sr, lambda o: res.__setitem__("bs", o))
                        for (a, b) in zip(res["gs"], res["bs"]):
                            c0, n, pg, apg = a
                            _, _, pbb, apb = b
                            act(tmpA[:, c0:c0 + n], apg, AF.Sigmoid, [pg], [B_tmpA])
                            tt(tmpB[:, c0:c0 + n], tmpA[:, c0:c0 + n], apb, MUL, [B_tmpA, pbb], [B_tmpB])
                        dense_tile(p, [(b2, v2, 0)], hr, lambda o: res.__setitem__("gh", o))
                        dense_tile(p, [(b4, v4, 0)], gr, lambda o: res.__setitem__("bh", o))
                        for (a, b) in zip(res["gh"], res["bh"]):
                            c0, n, pg, apg = a
                            _, _, pbb, apb = b
                            act(tmpA[:, c0:c0 + n], apg, AF.Sigmoid, [pg], [B_tmpA])
                            tt(tmpA[:, c0:c0 + n], tmpA[:, c0:c0 + n], apb, MUL, [B_tmpA, pbb], [B_tmpA])
                            tt(mg[:, dt, c0:c0 + n], tmpA[:, c0:c0 + n], tmpB[:, c0:c0 + n], ADD, [B_tmpA, B_tmpB], [B_mg[dt]])
                    mr = lambda k: (mg[:, k, :], B_mg[k])
                    for dt in range(16):
                        b1, v1 = wload(wout, 0, 16, dt * 128, 128)

                        def cons_m(outs, dt=dt):
                            for (c0, n, pb, ap) in outs:
                                copy(ACT, moT[:, dt, c0:c0 + n], ap, [pb], [B_mo[dt]])
                        dense_tile(p, [(b1, v1, 0)], mr, cons_m)
                    rms_rstd(p, moT, B_mo, 16)
                    g3 = gain(3)
                    for dt in range(16):
                        stt(tmpA[:, 0:NC_], moT[:, dt, 0:NC_], g3[:, dt:dt + 1], rstd[:, 0:NC_], MUL, MUL,
                            [B_mo[dt], B_rstd, B_gains], [B_tmpA])
                        tt(xT[:, dt, 0:NC_], xT[:, dt, 0:NC_], tmpA[:, 0:NC_], ADD, [B_tmpA, B_x[dt]], [B_x[dt]])
                    barrier()
                    del wslots[-n_extra:]
                barrier()

        prep_params()
        for a in range(na):
            pa_ = 10 + a
            load_x(pa_, xa, a * TB)
            ffn(pa_, f1g, f1u, f1d, 0, 0)
            mixer(pa_, False, so=True)
        for p in range(npass):
            load_x(p, xp, p * TB)
            if stage >= 1:
                ffn(p, f1g, f1u, f1d, 0, 0)
            if stage >= 2:
                mixer(p, p == npass - 1)
            if stage >= 3:
                ffn(p, f2g, f2u, f2d, 4, 16)
            store_y(p, p * TB)

        print("COUNTS", "PE", PE.cnt, "ACT", ACT.cnt, "DVE", DVE.cnt, "POOL", POOL.cnt, "dma", [c.cnt // 16 for c in chans])
        for c in chans:
            if c.cnt:
                nc.sync.wait_ge(c.sem, c.cnt)
        for E in (PE, ACT, DVE, POOL):
            if E.cnt:
                nc.sync.wait_ge(E.sem, E.cnt)
    return nc


_NC_CACHE = {}


def _fm(v, nt):
    return np.ascontiguousarray(np.asarray(v, np.float32).reshape(nt, 128).T)


def kernel(**inp):
    f32 = np.float32
    g = lambda k: np.asarray(inp[k], f32)
    x_prompt = g("x_prompt"); x_sample = g("x_sample")
    gains = np.concatenate([_fm(g(k)[0], 16) for k in
                            ("ffn1_pre_norm", "ffn1_post_norm", "mix_pre_norm", "mix_post_norm",
                             "ffn2_pre_norm", "ffn2_post_norm")], axis=1)
    lbl = g("hgrn_lb_logits")
    misc = np.concatenate([_fm(g("hgrn_out_norm")[0], 8), _fm(lbl[0], 8), _fm(lbl[1], 8),
                           _fm(g("s5_d")[0], 8), _fm(g("s5_b_glu")[0], 8)], axis=1)
    lam = np.concatenate([_fm(g("s5_lambda_re")[0].reshape(-1), 32), _fm(g("s5_lambda_im")[0].reshape(-1), 32),
                          _fm(np.repeat(g("s5_log_dt")[0], 64), 32)], axis=1)
    bre = g("s5_b_re")[0]; bim = g("s5_b_im")[0]
    cre = g("s5_c_re")[0]; cim = g("s5_c_im")[0]
    bblk = np.zeros((8, 128, 1024), f32)
    cblk = np.zeros((8, 4, 128, 256), f32)
    for gt in range(8):
        for gl in range(8):
            G_ = gt * 8 + gl
            bblk[gt, gl * 16:(gl + 1) * 16, gl * 64:(gl + 1) * 64] = bre[G_].T
            bblk[gt, gl * 16:(gl + 1) * 16, 512 + gl * 64:512 + (gl + 1) * 64] = bim[G_].T
            j, g2 = gl // 2, gl % 2
            cblk[gt, j, g2 * 64:(g2 + 1) * 64, gl * 16:(gl + 1) * 16] = cre[G_].T
            cblk[gt, j, g2 * 64:(g2 + 1) * 64, 128 + gl * 16:128 + (gl + 1) * 16] = cim[G_].T
    ident = np.eye(128, dtype=f32)
    mask = np.triu(np.ones((128, 128), f32))
    scanm = np.ones((128, TB), f32); scanm[:, ::128] = 0.0
    oneh = np.zeros((NS, NS * 128), f32)
    for j in range(NS):
        oneh[j, j * 128:(j + 1) * 128] = 1.0
    common = {
        "f1g": g("ffn1_w_gate")[0], "f1u": g("ffn1_w_up")[0], "f1d": g("ffn1_w_down")[0],
        "f2g": g("ffn2_w_gate")[0], "f2u": g("ffn2_w_up")[0], "f2d": g("ffn2_w_down")[0],
        "win": g("w_in")[0], "wglu": g("s5_w_glu")[0], "wbs": g("w_branch_s5")[0],
        "wbh": g("w_branch_hgrn")[0], "wout": g("w_out")[0],
        "gains": gains, "misc": misc, "lam": lam, "bblk": bblk, "cblk": cblk,
        "ident": ident, "mask": mask, "scanm": scanm, "oneh": oneh,
    }
    sre = g("state_s5_re")[0].reshape(128, 4096); sim = g("state_s5_im")[0].reshape(128, 4096)
    shg = g("state_hgrn")[0]
    zero_a = np.zeros((NA * TB, D), f32)
    in_maps = []
    for c in range(8):
        m = dict(common)
        sq_, half = c // 2, c % 2
        m["xa"] = np.ascontiguousarray(x_prompt[sq_, 0:NA * TB]) if half == 1 else zero_a
        m["xp"] = np.ascontiguousarray(x_prompt[sq_, half * NB * TB:(half + 1) * NB * TB])
        m["xs"] = np.ascontiguousarray(x_sample[c * NS:(c + 1) * NS, 0])
        m["sre"] = np.ascontiguousarray(sre[c * NS:(c + 1) * NS])
        m["sim"] = np.ascontiguousarray(sim[c * NS:(c + 1) * NS])
        m["shg"] = np.ascontiguousarray(shg[c * NS:(c + 1) * NS])
        in_maps.append(m)
    if "nc" not in _NC_CACHE:
        _NC_CACHE["nc"] = build_program()
    res = run_bass_kernel_spmd(_NC_CACHE["nc"], in_maps, core_ids=list(range(8)))
    R = res.results
    _NC_CACHE["last"] = R
    y_p = np.stack([np.concatenate([R[2 * q]["yp"], R[2 * q + 1]["yp"]]) for q in range(4)]).astype(f32)
    y_s = np.concatenate([R[c]["ys"] for c in range(8)])[:, None, :].astype(f32)
    re_p = np.stack([R[2 * q + 1]["o_re_p"].reshape(64, 64) for q in range(4)])[None].astype(f32)
    im_p = np.stack([R[2 * q + 1]["o_im_p"].reshape(64, 64) for q in range(4)])[None].astype(f32)
    hg_p = np.stack([R[2 * q + 1]["o_hg_p"] for q in range(4)])[None].astype(f32)
    re_s = np.concatenate([R[c]["o_re_s"] for c in range(8)]).reshape(1, 128, 64, 64).astype(f32)
    im_s = np.concatenate([R[c]["o_im_s"] for c in range(8)]).reshape(1, 128, 64, 64).astype(f32)
    hg_s = np.concatenate([R[c]["o_hg_s"] for c in range(8)])[None].astype(f32)
    return (y_p, y_s, re_p, im_p, hg_p, re_s, im_s, hg_s)
```
